# Optimizing a Trainium2 kernel written in Bass

```python
import jax, jax.numpy as jnp
from jax import lax
import numpy as np

D_MODEL = 4096
BATCH = 1
SEQ = 8192
DEPTH = 4

CHUNK = 64
MIX_W = D_MODEL
A_W = MIX_W // 4
A_HEADS = 4
A_DV = A_W // A_HEADS
A_DK = A_DV // 2
A_QK = A_HEADS * A_DK
A_IN = 2 * A_QK + 2 * A_W
ROPE_BASE = 10000.0
A_NORM_EPS = 1e-6
B_W = MIX_W // 4
B_HEAD = 64
B_HEADS = B_W // B_HEAD
B_DECAY_RANK = 128
B_AAA_RANK = 128
B_SHIFT_W = 3 * B_W + B_DECAY_RANK + B_AAA_RANK
B_LN_EPS = 64e-5
C_W = MIX_W // 2
C_HEADS = 16
C_DH = C_W // C_HEADS
C_BACK = 8
C_REL_CLIP = 128
C_N_REL = (CHUNK - 1) + C_REL_CLIP + 1
C_IN = 4 * C_W
IN_W = A_IN + B_SHIFT_W + B_W + C_IN
RMS_EPS = 1e-6

kernel_name = "hybrid_retention_rwkv7_chunkattn_trunk"


def _split(z, sizes):
    return jnp.split(z, np.cumsum(np.array(sizes))[:-1].tolist(), axis=-1)


def rms_norm(x, g):
    xf = x.astype(jnp.float32)
    y = xf * lax.rsqrt(jnp.mean(xf * xf, axis=-1, keepdims=True) + RMS_EPS)
    return (y * g.astype(jnp.float32)).astype(x.dtype)


def rotary(x, pos):
    d = x.shape[-1]
    inv = 1.0 / (ROPE_BASE ** (jnp.arange(0, d, 2, dtype=jnp.float32) / d))
    ang = pos.astype(jnp.float32)[:, None] * inv[None, :]
    cos = jnp.cos(ang)[None, :, None, :]
    sin = jnp.sin(ang)[None, :, None, :]
    xf = x.astype(jnp.float32)
    x1, x2 = xf[..., 0::2], xf[..., 1::2]
    out = jnp.stack([x1 * cos - x2 * sin, x1 * sin + x2 * cos], axis=-1)
    return out.reshape(x.shape).astype(x.dtype)


def retention(q, k, v):
    B_, S_, H, dk = q.shape
    dv = v.shape[-1]
    nc = S_ // CHUNK
    f32 = jnp.float32
    log_g = jnp.log(1.0 - 2.0 ** (-5.0 - jnp.arange(H, dtype=f32)))
    idx = jnp.arange(CHUNK, dtype=f32)
    intra = jnp.exp(log_g[:, None, None] * jnp.abs(idx[:, None] - idx[None, :]))
    to_end = jnp.exp(log_g[:, None] * (CHUNK - 1 - idx)[None, :])
    from_start = jnp.exp(log_g[:, None] * (idx + 1.0)[None, :])
    chunk_decay = jnp.exp(log_g * CHUNK)
    qf = q.astype(f32).reshape(B_, nc, CHUNK, H, dk)
    kf = k.astype(f32).reshape(B_, nc, CHUNK, H, dk) * (dk ** -0.5)
    vf = v.astype(f32).reshape(B_, nc, CHUNK, H, dv)
    scores = jnp.einsum('bcnhd,bcmhd->bchnm', qf, kf) * intra
    o_intra = jnp.einsum('bchnm,bcmhe->bcnhe', scores, vf)
    u = jnp.einsum('bcmhd,bcmhe,hm->cbhde', kf, vf, to_end)

    def step(s, u_c):
        return chunk_decay[None, :, None, None] * s + u_c, s

    _, s_prev = lax.scan(step, jnp.zeros((B_, H, dk, dv), f32), u)
    o_inter = jnp.einsum('bcnhd,cbhde,hn->bcnhe', qf, s_prev, from_start)
    o = (o_intra + o_inter).reshape(B_, S_, H, dv)
    o = o * lax.rsqrt(jnp.mean(o * o, axis=-1, keepdims=True) + A_NORM_EPS)
    return o.reshape(B_, S_, H * dv).astype(q.dtype)


def token_shift(z):
    return jnp.pad(z, ((0, 0), (1, 0), (0, 0)))[:, :-1]


def _wkv7_step(s, inp):
    r_t, w_t, k_t, v_t, kk_t, a_t = inp
    sa = jnp.einsum('bhij,bhj->bhi', s, kk_t)
    s = (s * w_t[:, :, None, :]
         - sa[..., :, None] * (kk_t * a_t)[..., None, :]
         + v_t[..., :, None] * k_t[..., None, :])
    y = jnp.einsum('bhij,bhj->bhi', s, r_t)
    return s, y


def rwkv7_time_mix(hb, mu, w0, w_up, a0, a_up, k_k, k_a, r_k, ln_g, ln_b):
    B_, S_, _ = hb.shape
    f32 = jnp.float32
    hb = hb + (token_shift(hb) - hb) * mu
    r, k, v, wd, ad = _split(hb, (B_W, B_W, B_W, B_DECAY_RANK, B_AAA_RANK))
    log_w = -jnp.exp(-jax.nn.softplus(-(w0 + jnp.tanh(wd) @ w_up).astype(f32)) - 0.5)
    a = jax.nn.sigmoid((a0 + ad @ a_up).astype(f32))

    def heads(z):
        return z.astype(f32).reshape(B_, S_, B_HEADS, B_HEAD)

    kk = heads(k * k_k)
    kk = kk / jnp.maximum(jnp.sqrt(jnp.sum(kk * kk, axis=-1, keepdims=True)), 1e-12)
    k_mod = k.astype(f32) * (1.0 + (a - 1.0) * k_a.astype(f32))
    r_h, k_h, v_h, a_h, w_h = heads(r), heads(k_mod), heads(v), heads(a), heads(jnp.exp(log_w))
    xs = tuple(jnp.moveaxis(z, 1, 0) for z in (r_h, w_h, k_h, v_h, kk, a_h))
    s0 = jnp.zeros((B_, B_HEADS, B_HEAD, B_HEAD), f32)
    _, y = lax.scan(_wkv7_step, s0, xs)
    y = jnp.moveaxis(y, 0, 1)
    mean = jnp.mean(y, axis=-1, keepdims=True)
    var = jnp.mean((y - mean) ** 2, axis=-1, keepdims=True)
    y = ((y - mean) * lax.rsqrt(var + B_LN_EPS)).reshape(B_, S_, B_W)
    y = y * ln_g.astype(f32) + ln_b.astype(f32)
    bonus = jnp.sum(r_h * k_h * r_k.astype(f32), axis=-1, keepdims=True) * v_h
    return (y + bonus.reshape(B_, S_, B_W)).astype(hb.dtype)


def chunk_rel_attention(q, k, v, rel_bias):
    B_, S_, H, dh = q.shape
    nc = S_ // CHUNK
    band = C_BACK + 1
    qc = q.reshape(B_, nc, CHUNK, H, dh) * (dh ** -0.5)
    pad = ((0, 0), (C_BACK, 0), (0, 0), (0, 0), (0, 0))
    kc = jnp.pad(k.reshape(B_, nc, CHUNK, H, dh), pad)
    vc = jnp.pad(v.reshape(B_, nc, CHUNK, H, dh), pad)
    cidx = jnp.arange(nc)[:, None] + jnp.arange(band)[None, :]
    kb = kc[:, cidx].reshape(B_, nc, band * CHUNK, H, dh)
    vb = vc[:, cidx].reshape(B_, nc, band * CHUNK, H, dh)
    n = jnp.arange(CHUNK)
    j = jnp.arange(band)
    dist = (n[:, None, None] + (C_BACK - j)[None, :, None] * CHUNK
            - n[None, None, :]).reshape(CHUNK, band * CHUNK)
    rel_idx = jnp.clip(dist, -(CHUNK - 1), C_REL_CLIP) + (CHUNK - 1)
    bias = rel_bias[:, rel_idx].astype(jnp.float32)
    valid = (jnp.arange(nc)[:, None] - C_BACK + j[None, :]) >= 0
    valid = jnp.repeat(valid, CHUNK, axis=1)
    s = jnp.einsum('bcqhd,bckhd->bchqk', qc, kb).astype(jnp.float32) + bias[None, None]
    s = jnp.where(valid[None, :, None, None, :], s, -1e30)
    p = jax.nn.softmax(s, axis=-1).astype(v.dtype)
    o = jnp.einsum('bchqk,bckhd->bcqhd', p, vb)
    return o.reshape(B_, S_, H * dh)


def setup_inputs(seed: int = 0) -> dict:
    key = jax.random.key(seed)
    ks = jax.random.split(key, 16)
    f32 = jnp.float32
    L = DEPTH
    x = jax.random.normal(ks[0], (BATCH, SEQ, D_MODEL), f32)
    norm_g = 1.0 + 0.02 * jax.random.normal(ks[1], (L, D_MODEL), f32)
    w_in = jax.random.normal(ks[2], (L, D_MODEL, IN_W), f32) * (D_MODEL ** -0.5)
    w_out = jax.random.normal(ks[3], (L, MIX_W, D_MODEL), f32) * (MIX_W ** -0.5)
    b_mu = jax.random.uniform(ks[4], (L, B_SHIFT_W), f32)
    b_w0 = jnp.linspace(-6.0, -1.0, B_W, dtype=f32)[None, :] + 0.1 * jax.random.normal(ks[5], (L, B_W), f32)
    b_w_up = jax.random.normal(ks[6], (L, B_DECAY_RANK, B_W), f32) * (0.1 * B_DECAY_RANK ** -0.5)
    b_a0 = 0.1 * jax.random.normal(ks[7], (L, B_W), f32)
    b_a_up = jax.random.normal(ks[8], (L, B_AAA_RANK, B_W), f32) * (0.5 * B_AAA_RANK ** -0.5)
    b_k_k = 0.85 + 0.05 * jax.random.normal(ks[9], (L, B_W), f32)
    b_k_a = 1.0 + 0.05 * jax.random.normal(ks[10], (L, B_W), f32)
    b_r_k = 0.1 * jax.random.normal(ks[11], (L, B_HEADS, B_HEAD), f32)
    b_ln_g = 1.0 + 0.02 * jax.random.normal(ks[12], (L, B_W), f32)
    b_ln_b = 0.02 * jax.random.normal(ks[13], (L, B_W), f32)
    c_rel_bias = 0.1 * jax.random.normal(ks[14], (L, C_HEADS, C_N_REL), f32)
    final_g = 1.0 + 0.02 * jax.random.normal(ks[15], (D_MODEL,), f32)
    return {"x": x, "norm_g": norm_g, "w_in": w_in, "w_out": w_out, "b_mu": b_mu,
            "b_w0": b_w0, "b_w_up": b_w_up, "b_a0": b_a0, "b_a_up": b_a_up,
            "b_k_k": b_k_k, "b_k_a": b_k_a, "b_r_k": b_r_k, "b_ln_g": b_ln_g,
            "b_ln_b": b_ln_b, "c_rel_bias": c_rel_bias, "final_g": final_g}


def reference(x, norm_g, w_in, w_out, b_mu, b_w0, b_w_up, b_a0, b_a_up, b_k_k, b_k_a,
              b_r_k, b_ln_g, b_ln_b, c_rel_bias, final_g):
    B_, S_ = x.shape[0], x.shape[1]
    pos = jnp.arange(S_)
    for l in range(DEPTH):
        h = rms_norm(x, norm_g[l]) @ w_in[l]
        ha, hb, gb, hc = _split(h, (A_IN, B_SHIFT_W, B_W, C_IN))
        qa, ka, va, ga = _split(ha, (A_QK, A_QK, A_W, A_W))
        qa = rotary(qa.reshape(B_, S_, A_HEADS, A_DK), pos)
        ka = rotary(ka.reshape(B_, S_, A_HEADS, A_DK), pos)
        ya = retention(qa, ka, va.reshape(B_, S_, A_HEADS, A_DV))
        yb = rwkv7_time_mix(hb, b_mu[l], b_w0[l], b_w_up[l], b_a0[l], b_a_up[l],
                            b_k_k[l], b_k_a[l], b_r_k[l], b_ln_g[l], b_ln_b[l])
        qc, kc, vc, gc = _split(hc, (C_W, C_W, C_W, C_W))
        yc = chunk_rel_attention(qc.reshape(B_, S_, C_HEADS, C_DH),
                                 kc.reshape(B_, S_, C_HEADS, C_DH),
                                 vc.reshape(B_, S_, C_HEADS, C_DH), c_rel_bias[l])
        mixed = jnp.concatenate([ya * jax.nn.silu(ga), yb * jax.nn.silu(gb),
                                 yc * jax.nn.silu(gc)], axis=-1)
        x = x + mixed @ w_out[l]
    return rms_norm(x, final_g)
```

```python
import numpy as np
import concourse.bass as bass
import concourse.mybir as mybir
from concourse.bass_utils import run_bass_kernel_spmd

F32 = mybir.dt.float32
BF16 = mybir.dt.bfloat16
ALU = mybir.AluOpType
AF = mybir.ActivationFunctionType
EPS = 1e-6

class Sem:
    def __init__(self, nc, name):
        self.cm = nc.semaphore(name)
        self.h = self.cm.__enter__()
        self.n = 0


class Prog:
    def __init__(self, nc):
        self.nc = nc
        self.q = {k: [] for k in ("pe", "act", "dve", "pool", "sp")}

    def op(self, eng, fn, waits=(), inc=None, by=1):
        waits = [(s.h, v) for (s, v) in waits if v > 0]
        val = None
        if inc is not None:
            inc.n += by
            val = inc.n
        ih = inc.h if inc is not None else None

        def run(e):
            for (h, v) in waits:
                e.wait_ge(h, v)
            ins = fn(e)
            if ih is not None:
                ins.then_inc(ih, by)
        self.q[eng].append(run)
        return val

    def emit(self):
        nc = self.nc
        with nc.Block() as block:
            @block.tensor
            def _(e):
                for f in self.q["pe"]:
                    f(e)

            @block.scalar
            def _(e):
                for f in self.q["act"]:
                    f(e)

            @block.vector
            def _(e):
                for f in self.q["dve"]:
                    f(e)

            @block.gpsimd
            def _(e):
                for f in self.q["pool"]:
                    f(e)

            @block.sync
            def _(e):
                for f in self.q["sp"]:
                    f(e)


A_QK, A_W, B_W, C_W = 512, 1024, 1024, 2048
O_QA, O_KA, O_VA, O_GA = 0, 512, 1024, 2048
O_R, O_K, O_V, O_WD, O_AD = 3072, 4096, 5120, 6144, 6272
O_GB = 6400
O_QC, O_KC, O_VC, O_GC = 7424, 9472, 11520, 13568
IN_W = 15616

def core_cols(c):
    ha, half = c // 2, c % 2
    d = np.arange(128)
    ev, od = d[0::2], d[1::2]
    eo = np.concatenate([ev, od]); oe = np.concatenate([od, ev])
    out = {}
    out["qa_eo"] = O_QA + ha * 128 + eo; out["qa_oe"] = O_QA + ha * 128 + oe
    out["ka_eo"] = O_KA + ha * 128 + eo; out["ka_oe"] = O_KA + ha * 128 + oe
    out["va"] = O_VA + ha * 256 + np.concatenate([half * 128 + np.arange(128), (1 - half) * 128 + np.arange(128)])
    out["ga"] = O_GA + ha * 256 + half * 128 + np.arange(128)
    b = c * 128 + np.arange(128)
    out["rb"] = O_R + b; out["kb"] = O_K + b; out["vb"] = O_V + b
    out["wd"] = O_WD + np.arange(128); out["ad"] = O_AD + np.arange(128)
    out["gb"] = O_GB + b
    cc = c * 256 + np.arange(256)
    out["qc"] = O_QC + cc; out["kc"] = O_KC + cc; out["vc"] = O_VC + cc; out["gc"] = O_GC + cc
    return out

def mixed_cols(c):
    ha, half = c // 2, c % 2
    return np.concatenate([ha * 256 + half * 128 + np.arange(128), 1024 + c * 128 + np.arange(128),
                           2048 + c * 256 + np.arange(256)])


D = 4096; KC = D // 128; NCOL = 2688
GROUPS = [(0, 896), (896, 768), (1664, 1024)]
_names = [("qa_eo",128),("qa_oe",128),("ka_eo",128),("ka_oe",128),("va",256),("ga",128),("rb",128),("kb",128),("vb",128),
          ("wd",128),("ad",128),("gb",128),("qc",256),("kc",256),("vc",256),("gc",256)]
OFF = {}; _o = 0
for _n, _w in _names: OFF[_n] = _o; _o += _w
assert _o == NCOL
QSCALE = 128 ** -0.5
SCAN_WAITS = False

class Ctx:
    def __init__(self, nc, P):
        self.nc, self.P = nc, P
        self.sems, self.psums, self.drams = {}, {}, {}
    def sem(self, name):
        if name not in self.sems: self.sems[name] = Sem(self.nc, name)
        return self.sems[name]
    def psum(self, name, shape, dt):
        if name not in self.psums: self.psums[name] = self.nc.psum_tensor(name, shape, dt).__enter__()
        return self.psums[name]
    def dram(self, name, shape, dt):
        if name not in self.drams: self.drams[name] = self.nc.dram_tensor(name, shape, dt)
        return self.drams[name]


def build_B(S, stage=1, ctx=None, io=None, tag=""):
    fused = ctx is not None
    nc = ctx.nc if fused else bass.Bass("TRN2", target_bir_lowering=False)
    NTT = S // 512
    def EXT(name, shape, dt, out=False):
        if fused: return io[name]
        return nc.dram_tensor(name, shape, dt, kind="ExternalOutput" if out else "ExternalInput")
    def SCR(name, shape, dt):
        return ctx.dram("B_" + name, shape, dt) if fused else nc.dram_tensor(name, shape, dt, kind="Internal")
    if fused:
        wc = io["wc"]; xnT_tile = io["xnT"]
    else:
        xnT = nc.dram_tensor("xnT", [D, S], BF16, kind="ExternalInput")
        wc = nc.dram_tensor("wc", [D, NCOL], F32, kind="ExternalInput")
        xnT_tile = lambda k, tt: xnT[k * 128:(k + 1) * 128, tt * 512:(tt + 1) * 512]
    hT = EXT("hT", [NCOL, S], F32, out=True) if (stage == 1 and not fused) else SCR("hT", [NCOL, S], F32)
    vtok = SCR("vtok", [S, 256], F32)
    vatok = SCR("vatok", [S, 256], F32)
    gtok = SCR("gtok", [S, 512], F32)
    ybtok = SCR("ybtok", [S, 128], F32)
    if stage >= 5:
        mixed = SCR("mixed", [S, 512], F32) if fused else EXT("mixed", [S, 512], F32, out=True)
    if stage >= 4:
        bpd = EXT("bp", [128, 12], F32)
        wupd = EXT("wup", [128, 128], F32)
        aupd = EXT("aup", [128, 128], F32)
        BDd = EXT("BD", [128, 128], F32)
        idBd = EXT("identB", [128, 128], F32)
        ybT = EXT("ybT", [128, S], F32, out=True) if (stage == 4 and not fused) else SCR("ybT", [128, S], F32)
    if stage >= 3:
        CSd = EXT("CS", [128, S], F32); SSd = EXT("SS", [128, S], F32)
        FSd = EXT("FS", [128, 128], F32)
        M2d = EXT("M2", [128, 128], F32)
        tgd = EXT("tg", [128, 2], F32)
        idd = EXT("identA", [128, 128], F32)
        ya = EXT("ya", [S, 256], F32, out=True) if (stage == 3 and not fused) else SCR("ya", [S, 256], F32)
    if stage >= 2:
        biasT = EXT("biasT", [128, 2, 5, 128], F32)
        yc = EXT("yc", [S, 256], F32, out=True) if (stage == 2 and not fused) else SCR("yc", [S, 256], F32)
    P = ctx.P if fused else Prog(nc)
    Sm = (lambda n: ctx.sem("B_" + n)) if fused else (lambda n: Sem(nc, n))
    _phase = []
    def A(name, shape, dt):
        cm = nc.sbuf_tensor(name + tag, shape, dt); t_ = cm.__enter__(); _phase.append(cm); return t_
    def end_phase():
        while _phase:
            _phase.pop().__exit__(None, None, None)
    def phase_barrier(waitlist):
        for q_ in ("pe", "act", "dve", "pool", "sp"):
            P.op(q_, lambda e: e.nop(), waits=list(waitlist))
    if fused:
        banks = [ctx.psum(f"bank{i}", [128, 512], F32) for i in range(8)]
        _pmap = {"ps0": banks[0], "ps1": banks[1]}
        Pm = lambda name, shape, dt: _pmap[name]
    else:
        Pm = lambda name, shape, dt: nc.psum_tensor(name, shape, dt).__enter__()
    if fused:
        phase_barrier(io["start"])
    wres = A("wres", [128, KC, 1024], BF16)
    xt = [A(f"xt{i}", [128, KC, 512], BF16) for i in range(2)]
    ot = [A(f"ot{i}", [128, 512], F32) for i in range(2)]
    ps = [Pm(f"ps{i}", [128, 512], F32) for i in range(2)]
    s_w = [Sm(f"w{g}") for g in range(len(GROUPS))]
    s_x = [Sm("x0"), Sm("x1")]; s_o = [Sm("o0"), Sm("o1")]
    s_mm = Sm("mm"); s_ev = Sm("ev")
    ev_vals = []; mm_last_of_xslot = [0, 0]; o_last = [0, 0]; mm_last_group = 0
    xi = 0; it = 0
    for g, (c0, ncol) in enumerate(GROUPS):
        for k in range(KC):
            v_w = P.op("pool", lambda e, k=k, c0=c0, ncol=ncol: e.dma_start(
                out=wres[:, k, 0:ncol], in_=wc[k * 128:(k + 1) * 128, c0:c0 + ncol]),
                waits=[(s_mm, mm_last_group)] if (k == 0 and mm_last_group) else [], inc=s_w[g], by=16)
        for tt in range(NTT):
            xb = xi % 2
            for k in range(KC):
                v_x = P.op("sp", lambda e, k=k, tt=tt, xb=xb: e.dma_start(
                    out=xt[xb][:, k, :], in_=xnT_tile(k, tt)),
                    waits=[(s_mm, mm_last_of_xslot[xb])] if (k == 0 and mm_last_of_xslot[xb]) else [],
                    inc=s_x[xb], by=16)
            for cb in range(ncol // 128):
                pb = it % 2
                for k in range(KC):
                    wts = []
                    if k == 0:
                        wts = [(s_w[g], v_w), (s_x[xb], v_x)]
                        if it >= 2:
                            wts.append((s_ev, ev_vals[it - 2]))
                    v_mm = P.op("pe", lambda e, pb=pb, k=k, cb=cb, xb=xb: e.matmul(
                        ps[pb][:], lhsT=wres[:, k, cb * 128:(cb + 1) * 128], rhs=xt[xb][:, k, :],
                        start=(k == 0), stop=(k == KC - 1)), waits=wts, inc=s_mm)
                wts = [(s_mm, v_mm)]
                if o_last[pb]:
                    wts.append((s_o[pb], o_last[pb]))
                sc_ = QSCALE if (OFF["qc"] <= c0 + cb * 128 < OFF["qc"] + 256 or OFF["ka_eo"] <= c0 + cb * 128 < OFF["ka_oe"] + 128) else 1.0
                v_ev = P.op("act", lambda e, pb=pb, sc_=sc_: e.activation(out=ot[pb][:], in_=ps[pb][:], func=AF.Copy, scale=sc_),
                            waits=wts, inc=s_ev)
                ev_vals.append(v_ev)
                r0 = c0 + cb * 128
                o_last[pb] = P.op("sp", lambda e, pb=pb, r0=r0, tt=tt: e.dma_start(
                    out=hT[r0:r0 + 128, tt * 512:(tt + 1) * 512], in_=ot[pb][:]),
                    waits=[(s_ev, v_ev)], inc=s_o[pb], by=16)
                it += 1
            tm_jobs = []
            if g == 0 and stage >= 3: tm_jobs.append((OFF["va"] - c0, 256, vatok, 0, AF.Copy))
            if g == 2 and stage >= 2: tm_jobs.append((OFF["vc"] - c0, 256, vtok, 0, AF.Copy))
            if stage >= 5:
                if g == 0: tm_jobs.append((OFF["ga"] - c0, 128, gtok, 0, AF.Silu))
                if g == 1: tm_jobs.append((OFF["gb"] - c0, 128, gtok, 128, AF.Silu))
                if g == 2: tm_jobs.append((OFF["gc"] - c0, 256, gtok, 256, AF.Silu))
            for (jc0, jw, jdst, jd0, jfn) in tm_jobs:
                for st4 in range(4):
                    pb = it % 2
                    for k in range(KC):
                        wts = []
                        if k == 0 and it >= 2:
                            wts.append((s_ev, ev_vals[it - 2]))
                        v_mm = P.op("pe", lambda e, pb=pb, k=k, xb=xb, st4=st4, jc0=jc0, jw=jw: e.matmul(
                            ps[pb][:, 0:jw], lhsT=xt[xb][:, k, st4 * 128:(st4 + 1) * 128], rhs=wres[:, k, jc0:jc0 + jw],
                            start=(k == 0), stop=(k == KC - 1)), waits=wts, inc=s_mm)
                    wts = [(s_mm, v_mm)]
                    if o_last[pb]:
                        wts.append((s_o[pb], o_last[pb]))
                    v_ev = P.op("act", lambda e, pb=pb, jw=jw, jfn=jfn: e.activation(out=ot[pb][:, 0:jw], in_=ps[pb][:, 0:jw], func=jfn),
                                waits=wts, inc=s_ev)
                    ev_vals.append(v_ev)
                    t0_ = tt * 512 + st4 * 128
                    o_last[pb] = P.op("sp", lambda e, pb=pb, t0_=t0_, jdst=jdst, jd0=jd0, jw=jw: e.dma_start(
                        out=jdst[t0_:t0_ + 128, jd0:jd0 + jw], in_=ot[pb][:, 0:jw]), waits=[(s_ev, v_ev)], inc=s_o[pb], by=16)
                    it += 1
            mm_last_of_xslot[xb] = v_mm
            xi += 1
        mm_last_group = v_mm

    end_phase()
    if stage >= 2:
        NQ = S // 128
        v_proj_done = [(s_o[0], s_o[0].n), (s_o[1], s_o[1].n)]
        v_proj_done += [(s_mm, s_mm.n), (s_ev, s_ev.n)]
        phase_barrier(v_proj_done)
        qT = A("qT", [128, 2, S], BF16); kT = A("kT", [128, 2, S], BF16)
        V1 = A("V1", [128, NQ, 2, 129], BF16); bT = A("bT", [128, 2, 5, 128], F32)
        s_cl = Sm("cl"); s_one = Sm("one")
        v_one = P.op("dve", lambda e: e.memset(V1[:, :, :, 128:129], 1.0), inc=s_one)
        for h in range(2):
            for c0_ in range(0, S, 2048):
                c1_ = min(S, c0_ + 2048)
                P.op("pool", lambda e, h=h, c0_=c0_, c1_=c1_: e.dma_start(out=qT[:, h, c0_:c1_], in_=hT[OFF["qc"] + h * 128:OFF["qc"] + (h + 1) * 128, c0_:c1_]),
                     waits=v_proj_done if (h == 0 and c0_ == 0) else [], inc=s_cl, by=16)
                P.op("pool", lambda e, h=h, c0_=c0_, c1_=c1_: e.dma_start(out=kT[:, h, c0_:c1_], in_=hT[OFF["kc"] + h * 128:OFF["kc"] + (h + 1) * 128, c0_:c1_]),
                     inc=s_cl, by=16)
        for j in range(NQ):
            P.op("pool", lambda e, j=j: e.dma_start(out=V1[:, j, :, 0:128],
                 in_=vtok[j * 128:(j + 1) * 128, :].rearrange("p (h d) -> p h d", h=2)),
                 waits=[(s_one, v_one)] if j == 0 else [], inc=s_cl, by=16)
        P.op("sp", lambda e: e.dma_start(out=bT[:], in_=biasT[:, :, :, :]), inc=s_cl, by=16)
        v_cl = s_cl.n
        if fused:
            c_psum_cms = []
            scA = [banks[2][:, :].rearrange("p (a b) -> p a b", a=4), banks[3][:, :].rearrange("p (a b) -> p a b", a=4)]
            scB = [banks[4][:, 0:128].rearrange("p (a b) -> p a b", a=1), banks[5][:, 0:128].rearrange("p (a b) -> p a b", a=1)]
            oP = [banks[6][:, 0:129], banks[7][:, 0:129]]
        else:
            c_psum_cms = [nc.psum_tensor(nm, shp, F32) for nm, shp in
                          [("scA0", [128, 4, 128]), ("scA1", [128, 4, 128]), ("scB0", [128, 1, 128]), ("scB1", [128, 1, 128]),
                           ("oP0", [128, 129]), ("oP1", [128, 129])]]
            _t = [cm.__enter__() for cm in c_psum_cms]
            scA, scB, oP = _t[0:2], _t[2:4], _t[4:6]
        sb = [A(f"sb{i}", [128, 5, 128], F32) for i in range(2)]
        pT = [A(f"pT{i}", [128, 5, 128], BF16) for i in range(2)]
        rc = [A(f"rc{i}", [128, 1], F32) for i in range(2)]
        yt = [A(f"yt{i}", [128, 128], F32) for i in range(2)]
        s_sc = Sm("sc"); s_ba = Sm("ba"); s_ex = Sm("ex"); s_pv = Sm("pv"); s_rc = Sm("rc"); s_y = Sm("y")
        s_ys = [Sm("ys0"), Sm("ys1")]
        ba_v, ex_v, pv_v, y_v, ys_v = [], [], [], [], [0, 0]
        n = 0
        for h in range(2):
            for i in range(NQ):
                b = n % 2
                slots = [(sl, i - 4 + sl) for sl in range(5) if i - 4 + sl >= 0]
                sA = [x for x in slots if x[0] < 4]; sBl = [x for x in slots if x[0] == 4]
                first = True
                for (sl, jt) in slots:
                    dst = scA[b][:, sl, :] if sl < 4 else scB[b][:, 0, :]
                    wts = []
                    if first:
                        wts = [(s_cl, v_cl)] if n == 0 else []
                        if n >= 2: wts.append((s_ba, ba_v[n - 2]))
                        first = False
                    v_sc = P.op("pe", lambda e, dst=dst, h=h, jt=jt, i=i: e.matmul(
                        dst, lhsT=kT[:, h, jt * 128:(jt + 1) * 128], rhs=qT[:, h, i * 128:(i + 1) * 128],
                        start=True, stop=True), waits=wts, inc=s_sc)
                lo = sA[0][0] if sA else 4
                wts = [(s_sc, v_sc)] + ([(s_ex, ex_v[n - 2])] if n >= 2 else [])
                if sA:
                    v_ba = P.op("dve", lambda e, b=b, lo=lo, h=h: e.tensor_tensor(
                        out=sb[b][:, lo:4, :], in0=scA[b][:, lo:4, :], in1=bT[:, h, lo:4, :], op=ALU.add), waits=wts, inc=s_ba)
                    wts = []
                v_ba = P.op("dve", lambda e, b=b, h=h: e.tensor_tensor(
                    out=sb[b][:, 4:5, :], in0=scB[b][:, 0:1, :], in1=bT[:, h, 4:5, :], op=ALU.add), waits=wts, inc=s_ba)
                ba_v.append(v_ba)
                wts = [(s_ba, v_ba)] + ([(s_pv, pv_v[n - 2])] if n >= 2 else [])
                v_ex = P.op("act", lambda e, b=b, lo=lo: e.activation(out=pT[b][:, lo:5, :], in_=sb[b][:, lo:5, :], func=AF.Exp),
                            waits=wts, inc=s_ex)
                ex_v.append(v_ex)
                for q_, (sl, jt) in enumerate(slots):
                    wts = []
                    if q_ == 0:
                        wts = [(s_ex, v_ex)] + ([(s_y, y_v[n - 2])] if n >= 2 else [])
                    v_pv = P.op("pe", lambda e, b=b, sl=sl, jt=jt, h=h, q_=q_, L=len(slots): e.matmul(
                        oP[b][:], lhsT=pT[b][:, sl, :], rhs=V1[:, jt, h, :], start=(q_ == 0), stop=(q_ == L - 1)),
                        waits=wts, inc=s_pv)
                pv_v.append(v_pv)
                v_rc = P.op("dve", lambda e, b=b: e.reciprocal(out=rc[b][:], in_=oP[b][:, 128:129]),
                            waits=[(s_pv, v_pv)] + ([(s_y, y_v[n - 2])] if n >= 2 else []), inc=s_rc)
                wts = [(s_rc, v_rc)] + ([(s_ys[b], ys_v[b])] if ys_v[b] else [])
                v_y = P.op("dve", lambda e, b=b: e.tensor_scalar(out=yt[b][:], in0=oP[b][:, 0:128], scalar1=rc[b][:, 0:1],
                                                                 scalar2=None, op0=ALU.mult), waits=wts, inc=s_y)
                y_v.append(v_y)
                ys_v[b] = P.op("sp", lambda e, b=b, i=i, h=h: e.dma_start(
                    out=yc[i * 128:(i + 1) * 128, h * 128:(h + 1) * 128], in_=yt[b][:]), waits=[(s_y, v_y)], inc=s_ys[b], by=16)
                n += 1
        P.op("sp", lambda e: e.nop(), waits=[(s_ys[0], s_ys[0].n), (s_ys[1], s_ys[1].n)])

    end_phase()
    if stage >= 3:
        NQ = S // 128
        a_start = [(s_o[0], s_o[0].n), (s_o[1], s_o[1].n)]
        if stage >= 2:
            a_start += [(s_ys[0], s_ys[0].n), (s_ys[1], s_ys[1].n), (s_pv, s_pv.n), (s_y, s_y.n)]
            a_start += [(s_sc, s_sc.n), (s_ba, s_ba.n), (s_ex, s_ex.n), (s_rc, s_rc.n), (s_cl, s_cl.n), (s_one, s_one.n)]
        phase_barrier(a_start)
        qr = A("qr", [128, S], BF16); kr = A("kr", [128, S], BF16); qs = A("qs", [128, S], BF16)
        Va = A("Va", [128, NQ, 256], BF16)
        M2 = A("M2s", [128, 128], F32); FS = A("FSs", [128, 128], F32); tg = A("tgs", [128, 2], F32)
        idf = A("idAf", [128, 128], F32); idb = A("idAb", [128, 128], BF16)
        s_ac = Sm("ac")
        for dst, src_ in ((M2, M2d), (FS, FSd), (tg, tgd), (idf, idd)):
            P.op("sp", lambda e, dst=dst, src_=src_: e.dma_start(out=dst[:], in_=src_[:, :]), waits=a_start if dst is M2 else [], inc=s_ac, by=16)
        for j in range(NQ):
            P.op("pool", lambda e, j=j: e.dma_start(out=Va[:, j, :], in_=vatok[j * 128:(j + 1) * 128, :]), inc=s_ac, by=16)
        v_ac = s_ac.n
        s_idb = Sm("idb")
        v_idb = P.op("dve", lambda e: e.tensor_copy(out=idb[:], in_=idf[:]), waits=[(s_ac, v_ac)], inc=s_idb)
        ta = A("ropeA", [128, 512], F32); tb = A("ropeB", [128, 512], F32); tcs = A("ropeC", [128, 512], F32); tss = A("ropeS", [128, 512], F32)
        t1 = A("ropeT1", [128, 512], F32); t2 = A("ropeT2", [128, 512], F32)
        s_rl = Sm("arl"); s_rd = Sm("ard")
        v_rd = 0
        for which, (ra, rb_, dstT) in enumerate((("qa_eo", "qa_oe", qr), ("ka_eo", "ka_oe", kr))):
            for tt in range(S // 512):
                cs_ = slice(tt * 512, (tt + 1) * 512)
                for dst, src_, r0 in ((ta, hT, OFF[ra]), (tb, hT, OFF[rb_]), (tcs, CSd, 0), (tss, SSd, 0)):
                    v_l = P.op("sp", lambda e, dst=dst, src_=src_, r0=r0, cs_=cs_: e.dma_start(out=dst[:], in_=src_[r0:r0 + 128, cs_]),
                               waits=([(s_rd, v_rd)] if v_rd else a_start) if dst is ta else [], inc=s_rl, by=16)
                v1 = P.op("dve", lambda e: e.tensor_tensor(out=t1[:], in0=ta[:], in1=tcs[:], op=ALU.mult), waits=[(s_rl, v_l)], inc=s_rd)
                v2 = P.op("dve", lambda e: e.tensor_tensor(out=t2[:], in0=tb[:], in1=tss[:], op=ALU.mult), waits=[(s_rd, v1)], inc=s_rd)
                v_rd = P.op("dve", lambda e, dstT=dstT, cs_=cs_: e.tensor_tensor(out=dstT[:, cs_], in0=t1[:], in1=t2[:], op=ALU.add),
                            waits=[(s_rd, v2)], inc=s_rd)
                if which == 0:
                    for u in range(4):
                        v_rd = P.op("dve", lambda e, tt=tt, u=u: e.tensor_tensor(
                            out=qs[:, tt * 512 + u * 128: tt * 512 + (u + 1) * 128], in0=qr[:, tt * 512 + u * 128: tt * 512 + (u + 1) * 128],
                            in1=FS[:], op=ALU.mult), waits=[(s_rd, v_rd), (s_ac, v_ac)], inc=s_rd)
        v_rope = v_rd
        Sf = A("Sf", [128, 256], F32); Sb = A("Sb", [128, 256], BF16)
        pTa = A("pTa", [128, 128], BF16); kte = A("kte", [128, 128], BF16)
        sqj = A("sqj", [128, 256], F32); ssA = A("ssA", [128, 1], F32); rsA = A("rsA", [128, 1], F32); yA = A("yA", [128, 256], F32)
        tpv = ps[1][:, 384:448].bitcast(BF16)
        s_a = Sm("a")
        s_ayst = Sm("ayst")
        v = P.op("dve", lambda e: e.memset(Sf[:], 0.0), waits=[(s_rd, v_rope)], inc=s_a)
        v = P.op("dve", lambda e: e.memset(Sb[:], 0.0), waits=[(s_a, v)], inc=s_a)
        v_st = 0
        for i in range(NQ):
            cs_ = slice(i * 128, (i + 1) * 128)
            v = P.op("pe", lambda e, cs_=cs_: e.matmul(ps[1][:, 0:128], lhsT=kr[:, cs_], rhs=qr[:, cs_], start=True, stop=True),
                     waits=[(s_a, v), (s_idb, v_idb)], inc=s_a)
            v = P.op("dve", lambda e: e.tensor_tensor(out=pTa[:], in0=ps[1][:, 0:128], in1=M2[:], op=ALU.mult), waits=[(s_a, v)], inc=s_a)
            v = P.op("pe", lambda e, i=i: e.matmul(ps[0][:, 0:256], lhsT=pTa[:], rhs=Va[:, i, :], start=True, stop=False), waits=[(s_a, v)], inc=s_a)
            v = P.op("pe", lambda e, cs_=cs_: e.matmul(ps[0][:, 0:256], lhsT=qs[:, cs_], rhs=Sb[:], start=False, stop=True), waits=[(s_a, v)], inc=s_a)
            v = P.op("act", lambda e: e.activation(out=sqj[:], in_=ps[0][:, 0:256], func=AF.Square, accum_out=ssA[:]), waits=[(s_a, v)], inc=s_a)
            v = P.op("dve", lambda e: e.tensor_scalar(out=rsA[:], in0=ssA[:], scalar1=1.0 / 256, scalar2=1e-6, op0=ALU.mult, op1=ALU.add), waits=[(s_a, v)], inc=s_a)
            v = P.op("act", lambda e: e.activation(out=rsA[:], in_=rsA[:], func=AF.Sqrt), waits=[(s_a, v)], inc=s_a)
            v = P.op("dve", lambda e: e.reciprocal(out=rsA[:], in_=rsA[:]), waits=[(s_a, v)], inc=s_a)
            v = P.op("dve", lambda e: e.tensor_scalar(out=yA[:], in0=ps[0][:, 0:256], scalar1=rsA[:, 0:1], scalar2=None, op0=ALU.mult),
                     waits=[(s_a, v)] + ([(s_ayst, v_st)] if v_st else []), inc=s_a)
            v_st = P.op("sp", lambda e, cs_=cs_: e.dma_start(out=ya[cs_, :], in_=yA[:]), waits=[(s_a, v)], inc=s_ayst, by=16)
            v = P.op("pe", lambda e, cs_=cs_: e.transpose(out=tpv, in_=kr[:, cs_], identity=idb[:]), waits=[(s_a, v)], inc=s_a)
            v = P.op("act", lambda e: e.activation(out=kte[:], in_=tpv, func=AF.Copy, scale=tg[:, 0:1]), waits=[(s_a, v)], inc=s_a)
            v = P.op("pe", lambda e, i=i: e.matmul(ps[1][:, 128:384], lhsT=kte[:], rhs=Va[:, i, :], start=True, stop=True), waits=[(s_a, v)], inc=s_a)
            v = P.op("dve", lambda e: e.scalar_tensor_tensor(out=Sf[:], in0=Sf[:], scalar=tg[:, 1:2], in1=ps[1][:, 128:384], op0=ALU.mult, op1=ALU.add),
                     waits=[(s_a, v)], inc=s_a)
            v = P.op("act", lambda e: e.activation(out=Sb[:], in_=Sf[:], func=AF.Copy), waits=[(s_a, v)], inc=s_a)
        P.op("sp", lambda e: e.nop(), waits=[(s_ayst, s_ayst.n), (s_a, v)])

    end_phase()
    if stage >= 4:
        b_start = [(s_o[0], s_o[0].n), (s_o[1], s_o[1].n)]
        if stage >= 2 and "s_pv" in dir():
            pass
        b_start += [(s_ys[0], s_ys[0].n), (s_ys[1], s_ys[1].n), (s_pv, s_pv.n), (s_y, s_y.n), (s_ayst, s_ayst.n), (s_a, s_a.n)]
        b_start += [(s_rd, s_rd.n), (s_ac, s_ac.n), (s_idb, s_idb.n), (s_rl, s_rl.n)]
        phase_barrier(b_start)
        for cm in reversed(c_psum_cms):
            cm.__exit__(None, None, None)
        if fused:
            bc = [banks[2 + i][:, 0:320].rearrange("p (k j) -> p k j", k=5) for i in range(4)]
        else:
            bc = [Pm(f"bc{i}", [128, 5, 64], F32) for i in range(4)]
        bp = A("bps", [128, 12], F32); wup = A("wups", [128, 128], F32); aup = A("aups", [128, 128], F32)
        BD = A("BDs", [128, 128], F32); idB = A("idB", [128, 128], F32)
        s_bl = Sm("bl")
        for dst, src_ in ((bp, bpd), (wup, wupd), (aup, aupd), (BD, BDd), (idB, idBd)):
            P.op("sp", lambda e, dst=dst, src_=src_: e.dma_start(out=dst[:], in_=src_[:, :]), waits=b_start if dst is bp else [], inc=s_bl, by=16)
        v_bl = s_bl.n
        TW = min(512, S); NSUB = TW // 128
        T = lambda nm, w=TW: A(nm, [128, w], F32)
        zin = {k: T("z_" + k, TW + 1) for k in ("r", "k", "v", "wd", "ad")}
        xs_ = {k: T("x_" + k) for k in ("r", "k", "v", "wd", "ad")}
        dtmp = T("dtmp"); wt = T("w_t"); at = T("a_t"); kkr = T("kkr"); sqt = T("sqt"); rn = T("rn"); kk = T("kk"); nkk = T("nkk"); bt = T("b_t")
        km = T("km"); rk = T("rk"); bon = T("bon")
        Xtok = A("Xtok", [128, NSUB, 5, 128], F32)
        St = A("St", [128, 64], F32); sa = A("sa", [128, 1], F32); ycol = T("ycol"); junk = A("junk", [128, 64], F32)
        dln = T("dln"); sqd = T("sqd"); rstd = T("rstd"); yo_ = T("yo_")
        s_zl = Sm("zl"); s_p = Sm("p"); s_bc = Sm("bcs"); s_d = Sm("d"); s_yst = Sm("yst")
        ytk = A("ytk", [128, NSUB, 128], F32); s_ytk = Sm("ytk"); v_ytk = 0
        rows = {"r": OFF["rb"], "k": OFF["kb"], "v": OFF["vb"], "wd": OFF["wd"], "ad": OFF["ad"]}
        mucol = {"r": 0, "k": 1, "v": 2, "wd": 3, "ad": 4}
        chain = [0]
        def step(eng, fn, extra=()):
            w = ([(s_p, chain[0])] if chain[0] else []) + list(extra)
            chain[0] = P.op(eng, fn, waits=w, inc=s_p)
            return chain[0]
        step("dve", lambda e: e.memset(St[:], 0.0), extra=[(s_bl, v_bl)])
        nstep = 0; v_yst = 0; d_hist = []
        for n in range(S // TW):
            t0 = n * TW
            for kname, z in zin.items():
                r0 = rows[kname]
                if n == 0:
                    step("dve", lambda e, z=z: e.memset(z[:, 0:1], 0.0))
                    v_l = P.op("sp", lambda e, z=z, r0=r0: e.dma_start(out=z[:, 1:TW + 1], in_=hT[r0:r0 + 128, 0:TW]),
                               waits=[(s_p, chain[0])], inc=s_zl, by=16)
                else:
                    v_l = P.op("sp", lambda e, z=z, r0=r0, t0=t0: e.dma_start(out=z[:, 0:TW + 1], in_=hT[r0:r0 + 128, t0 - 1:t0 + TW]),
                               waits=[(s_p, chain[0])], inc=s_zl, by=16)
            first = True
            for kname, z in zin.items():
                step("dve", lambda e, z=z: e.tensor_tensor(out=dtmp[:], in0=z[:, 0:TW], in1=z[:, 1:TW + 1], op=ALU.subtract),
                     extra=[(s_zl, v_l)] if first else []); first = False
                step("dve", lambda e, z=z, kname=kname: e.scalar_tensor_tensor(out=xs_[kname][:], in0=dtmp[:], scalar=bp[:, mucol[kname]:mucol[kname] + 1],
                                                                               in1=z[:, 1:TW + 1], op0=ALU.mult, op1=ALU.add))
            step("act", lambda e: e.activation(out=dtmp[:], in_=xs_["wd"][:], func=AF.Tanh))
            step("pe", lambda e: e.matmul(ps[0][:, 0:TW], lhsT=wup[:], rhs=dtmp[:], start=True, stop=True))
            step("act", lambda e: e.activation(out=wt[:], in_=ps[0][:, 0:TW], func=AF.Sigmoid, bias=bp[:, 5:6]))
            step("act", lambda e: e.activation(out=wt[:], in_=wt[:], func=AF.Exp, scale=-0.6065306597126334))
            step("pe", lambda e: e.matmul(ps[1][:, 0:TW], lhsT=aup[:], rhs=xs_["ad"][:], start=True, stop=True))
            step("act", lambda e: e.activation(out=at[:], in_=ps[1][:, 0:TW], func=AF.Sigmoid, bias=bp[:, 6:7]))
            step("dve", lambda e: e.tensor_scalar(out=kkr[:], in0=xs_["k"][:], scalar1=bp[:, 7:8], scalar2=None, op0=ALU.mult))
            step("dve", lambda e: e.tensor_tensor(out=sqt[:], in0=kkr[:], in1=kkr[:], op=ALU.mult))
            step("pe", lambda e: e.matmul(ps[0][:, 0:TW], lhsT=BD[:], rhs=sqt[:], start=True, stop=True))
            step("act", lambda e: e.activation(out=rn[:], in_=ps[0][:, 0:TW], func=AF.Sqrt))
            step("dve", lambda e: e.tensor_scalar(out=rn[:], in0=rn[:], scalar1=1e-12, scalar2=None, op0=ALU.max))
            step("dve", lambda e: e.reciprocal(out=rn[:], in_=rn[:]))
            step("dve", lambda e: e.tensor_tensor(out=kk[:], in0=kkr[:], in1=rn[:], op=ALU.mult))
            step("dve", lambda e: e.tensor_scalar(out=nkk[:], in0=kk[:], scalar1=-1.0, scalar2=None, op0=ALU.mult))
            step("dve", lambda e: e.tensor_tensor(out=bt[:], in0=kk[:], in1=at[:], op=ALU.mult))
            step("dve", lambda e: e.tensor_scalar(out=km[:], in0=at[:], scalar1=-1.0, scalar2=bp[:, 8:9], op0=ALU.add, op1=ALU.mult))
            step("dve", lambda e: e.scalar_tensor_tensor(out=km[:], in0=km[:], scalar=1.0, in1=xs_["k"][:], op0=ALU.add, op1=ALU.mult))
            step("dve", lambda e: e.scalar_tensor_tensor(out=rk[:], in0=xs_["r"][:], scalar=bp[:, 9:10], in1=km[:], op0=ALU.mult, op1=ALU.mult))
            step("pe", lambda e: e.matmul(ps[1][:, 0:TW], lhsT=BD[:], rhs=rk[:], start=True, stop=True))
            step("dve", lambda e: e.tensor_tensor(out=bon[:], in0=ps[1][:, 0:TW], in1=xs_["v"][:], op=ALU.mult))
            for kind, src_t in enumerate((nkk, wt, bt, km, xs_["r"])):
                pbk = ps[kind % 2]
                for sub in range(NSUB):
                    step("pe", lambda e, src_t=src_t, sub=sub, pbk=pbk: e.transpose(out=pbk[:, sub * 128:(sub + 1) * 128], in_=src_t[:, sub * 128:(sub + 1) * 128], identity=idB[:]))
                step("act", lambda e, kind=kind, pbk=pbk: e.activation(out=Xtok[:, :, kind, :], in_=pbk[:, 0:TW].rearrange("p (s c) -> p s c", s=NSUB), func=AF.Copy))
            v_prep = chain[0]
            for tt in range(TW):
                sub, t = tt // 128, tt % 128
                sl = nstep % 4
                sel = idB[:, t:t + 1].broadcast_to([128, 64])
                wts = [(s_p, v_prep)] if tt == 0 else []
                if nstep >= 4:
                    wts.append((s_d, d_hist[nstep - 4]))
                P.op("pe", lambda e, sl=sl, sel=sel, sub=sub: e.matmul(bc[sl][0:64, :, :], lhsT=sel, rhs=Xtok[:, sub, :, 0:64], start=True, stop=True), waits=wts, inc=s_bc)
                v_b = P.op("pe", lambda e, sl=sl, sel=sel, sub=sub: e.matmul(bc[sl][64:128, :, :], lhsT=sel, rhs=Xtok[:, sub, :, 64:128], start=True, stop=True), inc=s_bc)
                B_ = bc[sl]
                W = (lambda v_: [(s_d, v_)]) if SCAN_WAITS else (lambda v_: [])
                w_first = [(s_bc, v_b)] + ([(s_p, v_prep)] if tt == 0 else [])
                if SCAN_WAITS and s_d.n:
                    w_first.append((s_d, s_d.n))
                I = s_d if SCAN_WAITS else None
                v = P.op("dve", lambda e, B_=B_: e.scalar_tensor_tensor(out=junk[:], in0=St[:], scalar=1.0, in1=B_[:, 0, :],
                                                                        op0=ALU.mult, op1=ALU.mult, accum_out=sa[:]), waits=w_first, inc=I)
                v = P.op("dve", lambda e, B_=B_: e.tensor_tensor(out=St[:], in0=St[:], in1=B_[:, 1, :], op=ALU.mult), waits=W(v), inc=I)
                v = P.op("dve", lambda e, B_=B_: e.scalar_tensor_tensor(out=St[:], in0=B_[:, 2, :], scalar=sa[:, 0:1], in1=St[:], op0=ALU.mult, op1=ALU.add),
                         waits=W(v), inc=I)
                v = P.op("dve", lambda e, B_=B_, tt=tt: e.scalar_tensor_tensor(out=St[:], in0=B_[:, 3, :], scalar=xs_["v"][:, tt:tt + 1], in1=St[:], op0=ALU.mult, op1=ALU.add),
                         waits=W(v), inc=I)
                v = P.op("dve", lambda e, B_=B_, tt=tt: e.scalar_tensor_tensor(out=junk[:], in0=St[:], scalar=1.0, in1=B_[:, 4, :],
                                                                               op0=ALU.mult, op1=ALU.mult, accum_out=ycol[:, tt:tt + 1]),
                         waits=W(v), inc=s_d)
                d_hist.append(v); nstep += 1
            step("pe", lambda e: e.matmul(ps[0][:, 0:TW], lhsT=BD[:], rhs=ycol[:], start=True, stop=True), extra=[(s_d, d_hist[-1])])
            step("dve", lambda e: e.scalar_tensor_tensor(out=dln[:], in0=ps[0][:, 0:TW], scalar=-1.0 / 64, in1=ycol[:], op0=ALU.mult, op1=ALU.add))
            step("dve", lambda e: e.tensor_tensor(out=sqd[:], in0=dln[:], in1=dln[:], op=ALU.mult))
            step("pe", lambda e: e.matmul(ps[1][:, 0:TW], lhsT=BD[:], rhs=sqd[:], start=True, stop=True))
            step("dve", lambda e: e.tensor_scalar(out=rstd[:], in0=ps[1][:, 0:TW], scalar1=1.0 / 64, scalar2=64e-5, op0=ALU.mult, op1=ALU.add))
            step("act", lambda e: e.activation(out=rstd[:], in_=rstd[:], func=AF.Sqrt))
            step("dve", lambda e: e.reciprocal(out=rstd[:], in_=rstd[:]))
            step("dve", lambda e: e.tensor_tensor(out=dln[:], in0=dln[:], in1=rstd[:], op=ALU.mult))
            step("dve", lambda e: e.tensor_scalar(out=dln[:], in0=dln[:], scalar1=bp[:, 10:11], scalar2=bp[:, 11:12], op0=ALU.mult, op1=ALU.add))
            step("dve", lambda e: e.tensor_tensor(out=yo_[:], in0=dln[:], in1=bon[:], op=ALU.add), extra=[(s_yst, v_yst)] if v_yst else [])
            v_yst = P.op("sp", lambda e, t0=t0: e.dma_start(out=ybT[:, t0:t0 + TW], in_=yo_[:]), waits=[(s_p, chain[0])], inc=s_yst, by=16)
            if stage >= 5:
                for sub in range(NSUB):
                    step("pe", lambda e, sub=sub: e.transpose(out=ps[0][:, sub * 128:(sub + 1) * 128], in_=yo_[:, sub * 128:(sub + 1) * 128], identity=idB[:]))
                step("act", lambda e: e.activation(out=ytk[:], in_=ps[0][:, 0:TW].rearrange("p (s c) -> p s c", s=NSUB), func=AF.Copy),
                     extra=[(s_ytk, v_ytk)] if v_ytk else [])
                v_ytk = P.op("sp", lambda e, t0=t0: e.dma_start(out=ybtok[t0:t0 + TW, :].rearrange("(s p) c -> p s c", p=128), in_=ytk[:]),
                             waits=[(s_p, chain[0])], inc=s_ytk, by=16)
        P.op("sp", lambda e: e.nop(), waits=[(s_yst, s_yst.n)])

    if stage >= 5:
        f_start = [(s_yst, s_yst.n), (s_ytk, s_ytk.n), (s_p, chain[0]), (s_d, s_d.n), (s_bc, s_bc.n), (s_zl, s_zl.n), (s_bl, s_bl.n)]
        end_phase()
        phase_barrier(f_start)
        GB = min(4, S // 128)
        ycat = [A(f"ycat{i}", [128, GB, 512], F32) for i in range(2)]
        gt = [A(f"gt{i}", [128, GB, 512], F32) for i in range(2)]
        mo = [A(f"mo{i}", [128, GB, 512], F32) for i in range(2)]
        s_fl = [Sm("fl0"), Sm("fl1")]; s_fm = Sm("fm"); s_fs = [Sm("fs0"), Sm("fs1")]
        fm_v, fs_v = [], [0, 0]
        tm = lambda ap_: ap_.rearrange("(s p) c -> p s c", p=128)
        for n in range(S // (128 * GB)):
            b = n % 2; r_ = slice(n * 128 * GB, (n + 1) * 128 * GB)
            w0_ = [(s_fm, fm_v[n - 2])] if n >= 2 else []
            P.op("sp", lambda e, b=b, r_=r_: e.dma_start(out=ycat[b][:, :, 0:128], in_=tm(ya[r_, 0:128])), waits=w0_, inc=s_fl[b], by=16)
            P.op("sp", lambda e, b=b, r_=r_: e.dma_start(out=ycat[b][:, :, 128:256], in_=tm(ybtok[r_, :])), inc=s_fl[b], by=16)
            P.op("sp", lambda e, b=b, r_=r_: e.dma_start(out=ycat[b][:, :, 256:512], in_=tm(yc[r_, :])), inc=s_fl[b], by=16)
            v_l = P.op("sp", lambda e, b=b, r_=r_: e.dma_start(out=gt[b][:], in_=tm(gtok[r_, :])), inc=s_fl[b], by=16)
            v_m = P.op("dve", lambda e, b=b: e.tensor_tensor(out=mo[b][:], in0=ycat[b][:], in1=gt[b][:], op=ALU.mult),
                       waits=[(s_fl[b], v_l)] + ([(s_fs[b], fs_v[b])] if fs_v[b] else []), inc=s_fm)
            fm_v.append(v_m)
            fs_v[b] = P.op("sp", lambda e, b=b, r_=r_: e.dma_start(out=tm(mixed[r_, :]), in_=mo[b][:]), waits=[(s_fm, v_m)], inc=s_fs[b], by=16)
        P.op("sp", lambda e: e.nop(), waits=[(s_fs[0], s_fs[0].n), (s_fs[1], s_fs[1].n)])
    if fused:
        io["mixed"] = mixed
        io["done"] = [(s_fs[0], s_fs[0].n), (s_fs[1], s_fs[1].n), (s_fm, s_fm.n), (s_fl[0], s_fl[0].n), (s_fl[1], s_fl[1].n)]
        end_phase()
        return None
    P.op("sp", lambda e: e.nop(), waits=[(s_o[0], s_o[0].n), (s_o[1], s_o[1].n)])
    P.emit()
    return nc


RG4 = [[0, 1, 2, 3], [4, 5, 6, 7]]
RG2X = [[i, i + 4] for i in range(4)]


def shard_of_core(c):
    return 2 * c if c < 4 else 2 * (c - 4) + 1


def build_F(S, depth):
    nc = bass.Bass("TRN2", target_bir_lowering=False)
    T = S // 8
    NT = T // 128
    P = Prog(nc)
    ctx = Ctx(nc, P)
    ext = lambda name, shape, dt: nc.dram_tensor(name, shape, dt, kind="ExternalInput")
    x_sh = ext("x", [T, D], F32)
    gN = ext("gN", [depth + 1, D], F32)
    wc = ext("wc", [depth, D, NCOL], F32)
    wo = ext("wo", [depth, 512, D], F32)
    biasT = ext("biasT", [depth, 128, 2, 5, 128], F32)
    bp = ext("bp", [depth, 128, 12], F32)
    wup = ext("wup", [depth, 128, 128], F32)
    aup = ext("aup", [depth, 128, 128], F32)
    CS = ext("CS", [128, S], F32); SS = ext("SS", [128, S], F32)
    FS = ext("FS", [128, 128], F32); M2 = ext("M2", [128, 128], F32); tg = ext("tg", [128, 2], F32)
    ident = ext("ident", [128, 128], F32); BD = ext("BD", [128, 128], F32)
    out = nc.dram_tensor("out", [T, D], F32, kind="ExternalOutput")
    CA, NCA = 256, D // 256
    CW, NCR = 128, D // 128
    xnT_sh = nc.dram_tensor("xnT_sh", [D, T], BF16)
    ag2 = nc.dram_tensor("ag2", [NCA * 2 * CA, T], BF16)
    xnT_g = nc.dram_tensor("xnT_g", [NCA * 8 * CA, T], BF16)
    partial = nc.dram_tensor("partial", [NCR * S, CW], F32)
    rs4 = nc.dram_tensor("rs4", [NCR * (S // 4), CW], F32)
    delta = nc.dram_tensor("delta", [NCR * T, CW], F32)
    xcur = nc.dram_tensor("xcur", [T, D], F32, kind="Internal")

    banks = [ctx.psum(f"bank{i}", [128, 512], F32) for i in range(8)]
    Sm = lambda n: ctx.sem("F_" + n)
    _phase = []

    def A(name, shape, dt):
        cm = nc.sbuf_tensor(name, shape, dt); t_ = cm.__enter__(); _phase.append(cm); return t_

    def end_phase():
        while _phase:
            _phase.pop().__exit__(None, None, None)

    def phase_barrier(waitlist):
        for q_ in ("pe", "act", "dve", "pool", "sp"):
            P.op(q_, lambda e: e.nop(), waits=list(waitlist))

    s_cc = Sm("cc")

    def norm_phase(l, start):
        final = l == depth
        tg_ = f"_n{l}"
        phase_barrier(start)
        gb = A("gb" + tg_, [128, D], F32)
        idf = A("idf" + tg_, [128, 128], F32); idb = A("idb" + tg_, [128, 128], BF16)
        xr = [A(f"xr{i}" + tg_, [128, D], F32) for i in range(2)]
        dl = [A(f"dl{i}" + tg_, [128, D], F32) for i in range(2)]
        sq = A("sq" + tg_, [128, D], BF16)
        ss = [A(f"ss{i}" + tg_, [128, 1], F32) for i in range(2)]
        rs = [A(f"rs{i}" + tg_, [128, 1], F32) for i in range(2)]
        yo = [A(f"yo{i}" + tg_, [128, D], F32 if final else BF16) for i in range(2)]
        xT = [A(f"xT{i}" + tg_, [128, KC, 128], BF16) for i in range(2)]
        tpv = [banks[i][:, 0:256].bitcast(BF16).rearrange("p (a b) -> p a b", a=4) for i in range(2)]
        s_c = Sm("ncst"); s_l = [Sm("nl0"), Sm("nl1")]; s_e = Sm("ne"); s_x = [Sm("nx0"), Sm("nx1")]; s_y = [Sm("ny0"), Sm("ny1")]
        s_tp = Sm("ntp"); s_te = Sm("nte"); s_ts = [Sm("nts0"), Sm("nts1")]
        P.op("sp", lambda e: e.dma_start(out=gb[:], in_=gN[l:l + 1, :].broadcast_to([128, D])), inc=s_c, by=16)
        v_c = P.op("sp", lambda e: e.dma_start(out=idf[:], in_=ident[:, :]), inc=s_c, by=16)
        v_id = P.op("dve", lambda e: e.tensor_copy(out=idb[:], in_=idf[:]), waits=[(s_c, v_c)], inc=s_e)
        src = x_sh if l <= 1 else xcur
        e_hist = {}
        y1_v, xs_v, ys_v, te_v, ts_v = [], [0, 0], [0, 0], [], [0, 0]
        tev_last = [0, 0]; grp = 0
        for t in range(NT):
            b = t % 2; r_ = slice(t * 128, (t + 1) * 128)
            w_reuse = [(s_e, y1_v[t - 2])] if t >= 2 else []
            if xs_v[b]:
                w_reuse.append((s_x[b], xs_v[b]))
            v_l = P.op("sp", lambda e, b=b, r_=r_: e.dma_start(out=xr[b][:], in_=src[r_, :]), waits=w_reuse, inc=s_l[b], by=16)
            if l >= 1:
                v_l = P.op("sp", lambda e, b=b, r_=r_: e.dma_start(out=dl[b][:].rearrange("p (c w) -> p c w", c=NCR),
                           in_=delta.ap().rearrange("(c t) w -> t c w", c=NCR)[r_, :, :]), inc=s_l[b], by=16)
                w_ = [(s_l[b], v_l)] + ([(s_x[b], xs_v[b])] if xs_v[b] else [])
                v = P.op("dve", lambda e, b=b: e.tensor_tensor(out=xr[b][:], in0=xr[b][:], in1=dl[b][:], op=ALU.add), waits=w_, inc=s_e)
                if not final:
                    xs_v[b] = P.op("sp", lambda e, b=b, r_=r_: e.dma_start(out=xcur[r_, :], in_=xr[b][:]), waits=[(s_e, v)], inc=s_x[b], by=16)
                w_sq = [(s_e, v)]
            else:
                w_sq = [(s_l[b], v_l)]
            if t >= 1:
                w_sq.append((s_e, e_hist[("sq", t - 1)]))
            if t >= 2:
                w_sq.append((s_e, e_hist[("r1", t - 2)]))
            v = P.op("act", lambda e, b=b: e.activation(out=sq[:], in_=xr[b][:], func=AF.Square, accum_out=ss[b][:]), waits=w_sq, inc=s_e)
            e_hist[("sq", t)] = v
            v = P.op("dve", lambda e, b=b: e.tensor_scalar(out=rs[b][:], in0=ss[b][:], scalar1=1.0 / D, scalar2=EPS, op0=ALU.mult, op1=ALU.add),
                     waits=[(s_e, v)] + ([(s_e, y1_v[t - 2])] if t >= 2 else []), inc=s_e)
            e_hist[("r1", t)] = v
            v = P.op("act", lambda e, b=b: e.activation(out=rs[b][:], in_=rs[b][:], func=AF.Sqrt), waits=[(s_e, v)], inc=s_e)
            v = P.op("dve", lambda e, b=b: e.reciprocal(out=rs[b][:], in_=rs[b][:]), waits=[(s_e, v)], inc=s_e)
            w_ = [(s_e, v), (s_c, v_c)]
            if final and ys_v[b]: w_.append((s_y[b], ys_v[b]))
            if (not final) and t >= 2: w_.append((s_tp, e_hist[("tp", t - 2)]))
            if l >= 1 and not final and xs_v[b]: w_.append((s_x[b], xs_v[b]))
            v = P.op("dve", lambda e, b=b: e.scalar_tensor_tensor(out=yo[b][:], in0=xr[b][:], scalar=rs[b][:, 0:1], in1=gb[:], op0=ALU.mult, op1=ALU.mult),
                     waits=w_, inc=s_e)
            y1_v.append(v)
            if final:
                ys_v[b] = P.op("sp", lambda e, b=b, r_=r_: e.dma_start(out=out[r_, :], in_=yo[b][:]), waits=[(s_e, v)], inc=s_y[b], by=16)
                continue
            for k0 in range(0, KC, 4):
                pb = grp % 2
                for j in range(4):
                    w_ = []
                    if j == 0:
                        w_ = [(s_e, v), (s_e, v_id)]
                        if tev_last[pb]: w_.append((s_te, tev_last[pb]))
                    v_tp = P.op("pe", lambda e, pb=pb, j=j, b=b, k=k0 + j: e.transpose(out=tpv[pb][:, j, :], in_=yo[b][:, k * 128:(k + 1) * 128], identity=idb[:]),
                                waits=w_, inc=s_tp)
                w_ = [(s_tp, v_tp)]
                if k0 == 0 and ts_v[b]: w_.append((s_ts[b], ts_v[b]))
                v_te = P.op("act", lambda e, pb=pb, k0=k0, b=b: e.activation(out=xT[b][:, k0:k0 + 4, :], in_=tpv[pb][:, :, :], func=AF.Copy), waits=w_, inc=s_te)
                tev_last[pb] = v_te; grp += 1
            e_hist[("tp", t)] = v_tp
            ts_v[b] = P.op("sp", lambda e, b=b, t=t: e.dma_start(
                out=xnT_sh.ap().rearrange("(kc p) t -> p kc t", p=128)[:, :, t * 128:(t + 1) * 128], in_=xT[b][:]),
                waits=[(s_te, v_te)], inc=s_ts[b], by=16)
        done = [(s_e, s_e.n), (s_tp, s_tp.n), (s_te, s_te.n), (s_c, s_c.n)]
        for sl in (s_l, s_x, s_y, s_ts):
            done += [(sl[0], sl[0].n), (sl[1], sl[1].n)]
        end_phase()
        return done

    def coll(kind, op, rg, src_ap, dst_ap, start):
        v = P.op("pool", lambda e: e.collective_compute(kind, op, replica_groups=rg, ins=[src_ap.opt()], outs=[dst_ap.opt()]),
                 waits=list(start), inc=s_cc, by=1)
        return [(s_cc, v)]

    def allgather_xnT(start):
        d = start
        for j in range(NCA):
            d = coll("AllGather", ALU.bypass, RG2X, xnT_sh[j * CA:(j + 1) * CA, :], ag2[j * 2 * CA:(j + 1) * 2 * CA, :], d)
            d = coll("AllGather", ALU.bypass, RG4, ag2[j * 2 * CA:(j + 1) * 2 * CA, :], xnT_g[j * 8 * CA:(j + 1) * 8 * CA, :], d)
        return d

    def reducescatter_partial(start):
        d = start
        for c in range(NCR):
            d = coll("ReduceScatter", ALU.add, RG4, partial[c * S:(c + 1) * S, :], rs4[c * (S // 4):(c + 1) * (S // 4), :], d)
            d = coll("ReduceScatter", ALU.add, RG2X, rs4[c * (S // 4):(c + 1) * (S // 4), :], delta[c * T:(c + 1) * T, :], d)
        return d

    def oproj_phase(l, mixed, start):
        tg_ = f"_o{l}"
        phase_barrier(start)
        idf = A("idf" + tg_, [128, 128], F32); idb = A("idb" + tg_, [128, 128], BF16)
        wb = A("wb" + tg_, [128, 4, D], BF16)
        mf = [A(f"mf{i}" + tg_, [128, 512], F32) for i in range(2)]
        mb = [A(f"mb{i}" + tg_, [128, 512], BF16) for i in range(2)]
        mT = [A(f"mT{i}" + tg_, [128, 4, 128], BF16) for i in range(2)]
        ot = [A(f"ot{i}" + tg_, [128, 512], F32) for i in range(2)]
        tpv = [banks[i][:, 0:256].bitcast(BF16).rearrange("p (a b) -> p a b", a=4) for i in range(2)]
        pso = [banks[2], banks[3]]
        s_c = Sm("oc"); s_l = [Sm("ol0"), Sm("ol1")]; s_e = Sm("oe"); s_tp = Sm("otp"); s_mm = Sm("omm"); s_ev = Sm("oev"); s_s = [Sm("os0"), Sm("os1")]
        v_c = P.op("sp", lambda e: e.dma_start(out=idf[:], in_=ident[:, :]), inc=s_c, by=16)
        for k in range(4):
            for h0 in range(0, D, 2048):
                P.op("pool", lambda e, k=k, h0=h0: e.dma_start(out=wb[:, k, h0:h0 + 2048], in_=wo[l, k * 128:(k + 1) * 128, h0:h0 + 2048]), inc=s_c, by=16)
        v_w = s_c.n
        v_id = P.op("dve", lambda e: e.tensor_copy(out=idb[:], in_=idf[:]), waits=[(s_c, v_w)], inc=s_e)
        cast_v, tpl_v, tev_v, ev_v, st_v = [], [], [], [], [0, 0]
        it = 0
        for t in range(S // 128):
            b = t % 2; r_ = slice(t * 128, (t + 1) * 128)
            v_l = P.op("sp", lambda e, b=b, r_=r_: e.dma_start(out=mf[b][:], in_=mixed[r_, :]), waits=[(s_e, cast_v[t - 2])] if t >= 2 else [], inc=s_l[b], by=16)
            v_cst = P.op("dve", lambda e, b=b: e.tensor_copy(out=mb[b][:], in_=mf[b][:]),
                         waits=[(s_l[b], v_l)] + ([(s_tp, tpl_v[t - 2])] if t >= 2 else []), inc=s_e)
            cast_v.append(v_cst)
            for j in range(4):
                w_ = []
                if j == 0:
                    w_ = [(s_e, v_cst), (s_e, v_id)] + ([(s_e, tev_v[t - 2])] if t >= 2 else [])
                v_tp = P.op("pe", lambda e, b=b, j=j: e.transpose(out=tpv[b][:, j, :], in_=mb[b][:, j * 128:(j + 1) * 128], identity=idb[:]), waits=w_, inc=s_tp)
            tpl_v.append(v_tp)
            v_te = P.op("act", lambda e, b=b: e.activation(out=mT[b][:], in_=tpv[b][:, :, :], func=AF.Copy),
                        waits=[(s_tp, v_tp)] + ([(s_mm, mm_last[t - 2])] if t >= 2 else []), inc=s_e)
            tev_v.append(v_te)
            if t == 0:
                mm_last = {}
            for nb in range(D // 512):
                pb = it % 2
                for k in range(4):
                    w_ = []
                    if k == 0:
                        w_ = [(s_e, v_te), (s_c, v_w)] + ([(s_ev, ev_v[it - 2])] if it >= 2 else [])
                    v_mm = P.op("pe", lambda e, pb=pb, k=k, b=b, nb=nb: e.matmul(pso[pb][:], lhsT=mT[b][:, k, :], rhs=wb[:, k, nb * 512:(nb + 1) * 512],
                                                                           start=(k == 0), stop=(k == 3)), waits=w_, inc=s_mm)
                w_ = [(s_mm, v_mm)] + ([(s_s[pb], st_v[pb])] if st_v[pb] else [])
                eng = "act" if it % 2 == 0 else "dve"
                if eng == "act":
                    v_ev = P.op("act", lambda e, pb=pb: e.activation(out=ot[pb][:], in_=pso[pb][:], func=AF.Copy), waits=w_, inc=s_ev)
                else:
                    v_ev = P.op("dve", lambda e, pb=pb: e.tensor_copy(out=ot[pb][:], in_=pso[pb][:]), waits=w_, inc=s_ev)
                ev_v.append(v_ev)
                for i4 in range(512 // CW):
                    c_ = nb * (512 // CW) + i4
                    st_v[pb] = P.op("sp", lambda e, pb=pb, t=t, c_=c_, i4=i4: e.dma_start(
                        out=partial[c_ * S + t * 128: c_ * S + (t + 1) * 128, :], in_=ot[pb][:, i4 * CW:(i4 + 1) * CW]),
                        waits=[(s_ev, v_ev)] if i4 == 0 else [], inc=s_s[pb], by=16)
                it += 1
            mm_last[t] = v_mm
        done = [(s_e, s_e.n), (s_tp, s_tp.n), (s_mm, s_mm.n), (s_ev, s_ev.n), (s_c, s_c.n),
                (s_l[0], s_l[0].n), (s_l[1], s_l[1].n), (s_s[0], s_s[0].n), (s_s[1], s_s[1].n)]
        end_phase()
        return done

    done = []
    for l in range(depth):
        done = norm_phase(l, done)
        done = allgather_xnT(done)

        def xnT_tile(k, tt, _T=T):
            q, t0 = (tt * 512) // _T, (tt * 512) % _T
            j, r0 = k // 2, (k % 2) * 128
            base = (j * 8 + q) * CA + r0
            return xnT_g[base:base + 128, t0:t0 + 512]
        io = {"xnT": xnT_tile, "wc": wc[l], "biasT": biasT[l], "bp": bp[l], "wup": wup[l], "aup": aup[l], "CS": CS, "SS": SS, "FS": FS, "M2": M2,
              "tg": tg, "identA": ident, "identB": ident, "BD": BD, "start": done}
        build_B(S, 5, ctx=ctx, io=io, tag=f"_L{l}")
        done = oproj_phase(l, io["mixed"], io["done"])
        done = reducescatter_partial(done)
    done = norm_phase(depth, done)
    P.op("sp", lambda e: e.nop(), waits=done)
    P.emit()
    return nc


SEQ, DEPTH, NCORES = 8192, 4, 8
_CH, _BACK, _CLIP = 64, 8, 128


def _bias_tiles(rb):
    q = np.arange(128)[None, :]
    out = np.empty((5, 128, 128), np.float32)
    for j in range(5):
        k = (j - 4) * 128 + np.arange(128)[:, None]
        qc_, kc_ = q // _CH, np.floor_divide(k, _CH)
        valid = (kc_ <= qc_) & (kc_ >= qc_ - _BACK)
        out[j] = np.where(valid, rb[np.clip(q - k, -(_CH - 1), _CLIP) + (_CH - 1)], -30000.0)
    return out


_PROG = {}


def kernel(x, norm_g, w_in, w_out, b_mu, b_w0, b_w_up, b_a0, b_a_up, b_k_k, b_k_a, b_r_k, b_ln_g, b_ln_b, c_rel_bias, final_g):
    f = np.float32
    A_ = lambda a: np.asarray(a, f)
    x = A_(x); S = x.shape[1]; T = S // NCORES; L = DEPTH
    norm_g, w_in, w_out, b_mu, b_w0, b_w_up, b_a0, b_a_up = map(A_, (norm_g, w_in, w_out, b_mu, b_w0, b_w_up, b_a0, b_a_up))
    b_k_k, b_k_a, b_r_k, b_ln_g, b_ln_b, c_rel_bias, final_g = map(A_, (b_k_k, b_k_a, b_r_k, b_ln_g, b_ln_b, c_rel_bias, final_g))
    inv = (1.0 / (10000.0 ** (np.arange(0, 128, 2, dtype=f) / 128))).astype(f)
    ang = np.arange(S, dtype=f)[None, :] * inv[:, None]
    cc_, ss_ = np.cos(ang).astype(f), np.sin(ang).astype(f)
    CS, SS = np.concatenate([cc_, cc_], 0), np.concatenate([-ss_, ss_], 0)
    ident = np.eye(128, dtype=f); BD = np.kron(np.eye(2, dtype=f), np.ones((64, 64), f))
    gN = np.ascontiguousarray(np.concatenate([norm_g[:L], final_g[None]], 0))
    maps = []
    for c in range(NCORES):
        sh = shard_of_core(c); ha = c // 2
        lg = np.log(np.float64(1.0 - 2.0 ** (-5.0 - ha)))
        mm_ = np.arange(128)[:, None]; nn_ = np.arange(128)[None, :]; same = (mm_ // 64) == (nn_ // 64)
        ci = np.concatenate(list(core_cols(c).values())); mc = mixed_cols(c)
        ch = c * 128 + np.arange(128); hd = ch // 64; jj = ch % 64; r128 = np.arange(128)
        maps.append({
            "x": np.ascontiguousarray(x[0, sh * T:(sh + 1) * T]), "gN": gN,
            "wc": np.ascontiguousarray(w_in[:L][:, :, ci]), "wo": np.ascontiguousarray(w_out[:L][:, mc, :]),
            "biasT": np.ascontiguousarray(np.stack([np.stack([_bias_tiles(c_rel_bias[l][2 * c + hh]) for hh in range(2)], 0).transpose(2, 0, 1, 3)
                                                    for l in range(L)], 0)),
            "bp": np.stack([np.stack([b_mu[l][ch], b_mu[l][1024 + ch], b_mu[l][2048 + ch], b_mu[l][3072 + r128], b_mu[l][3200 + r128],
                                      b_w0[l][ch], b_a0[l][ch], b_k_k[l][ch], b_k_a[l][ch], b_r_k[l][hd, jj], b_ln_g[l][ch], b_ln_b[l][ch]], 1)
                            for l in range(L)], 0).astype(f),
            "wup": np.ascontiguousarray(b_w_up[:L][:, :, ch]), "aup": np.ascontiguousarray(b_a_up[:L][:, :, ch]),
            "CS": CS, "SS": SS,
            "FS": np.broadcast_to(np.exp(lg * (np.arange(128) + 1.0)).astype(f)[None, :], (128, 128)).copy(),
            "M2": np.where(same, np.exp(lg * np.abs(nn_ - mm_)), np.where(mm_ < nn_, np.exp(lg * (nn_ - mm_)), 0.0)).astype(f),
            "tg": np.stack([np.exp(lg * (127.0 - np.arange(128))), np.full(128, np.exp(lg * 128))], 1).astype(f),
            "ident": ident, "BD": BD})
    if (S, L) not in _PROG:
        _PROG[(S, L)] = build_F(S, L)
    r = run_bass_kernel_spmd(_PROG[(S, L)], maps, core_ids=list(range(NCORES)))
    out = np.empty((1, S, x.shape[2]), f)
    for c in range(NCORES):
        sh = shard_of_core(c)
        out[0, sh * T:(sh + 1) * T] = r.results[c]["out"]
    return out
```

```python
import numpy as np
import concourse.bass as bass
import concourse.mybir as mybir
from concourse.bass_utils import run_bass_kernel_spmd

F32 = mybir.dt.float32
BF16 = mybir.dt.bfloat16
ALU = mybir.AluOpType
AF = mybir.ActivationFunctionType
EPS = 1e-6

class Sem:
    def __init__(self, nc, name):
        self.cm = nc.semaphore(name)
        self.h = self.cm.__enter__()
        self.n = 0


class Prog:
    def __init__(self, nc):
        self.nc = nc
        self.q = {k: [] for k in ("pe", "act", "dve", "pool", "sp")}

    def op(self, eng, fn, waits=(), inc=None, by=1):
        waits = [(s.h, v) for (s, v) in waits if v > 0]
        val = None
        if inc is not None:
            inc.n += by
            val = inc.n
        ih = inc.h if inc is not None else None

        def run(e):
            for (h, v) in waits:
                e.wait_ge(h, v)
            ins = fn(e)
            if ih is not None:
                ins.then_inc(ih, by)
        self.q[eng].append(run)
        return val

    def emit(self):
        nc = self.nc
        with nc.Block() as block:
            @block.tensor
            def _(e):
                for f in self.q["pe"]:
                    f(e)

            @block.scalar
            def _(e):
                for f in self.q["act"]:
                    f(e)

            @block.vector
            def _(e):
                for f in self.q["dve"]:
                    f(e)

            @block.gpsimd
            def _(e):
                for f in self.q["pool"]:
                    f(e)

            @block.sync
            def _(e):
                for f in self.q["sp"]:
                    f(e)


A_QK, A_W, B_W, C_W = 512, 1024, 1024, 2048
O_QA, O_KA, O_VA, O_GA = 0, 512, 1024, 2048
O_R, O_K, O_V, O_WD, O_AD = 3072, 4096, 5120, 6144, 6272
O_GB = 6400
O_QC, O_KC, O_VC, O_GC = 7424, 9472, 11520, 13568
IN_W = 15616

def core_cols(c):
    ha, half = c // 2, c % 2
    d = np.arange(128)
    ev, od = d[0::2], d[1::2]
    eo = np.concatenate([ev, od]); oe = np.concatenate([od, ev])
    out = {}
    out["qa_eo"] = O_QA + ha * 128 + eo; out["qa_oe"] = O_QA + ha * 128 + oe
    out["ka_eo"] = O_KA + ha * 128 + eo; out["ka_oe"] = O_KA + ha * 128 + oe
    out["va"] = O_VA + ha * 256 + np.concatenate([half * 128 + np.arange(128), (1 - half) * 128 + np.arange(128)])
    out["ga"] = O_GA + ha * 256 + half * 128 + np.arange(128)
    b = c * 128 + np.arange(128)
    out["rb"] = O_R + b; out["kb"] = O_K + b; out["vb"] = O_V + b
    out["wd"] = O_WD + np.arange(128); out["ad"] = O_AD + np.arange(128)
    out["gb"] = O_GB + b
    cc = c * 256 + np.arange(256)
    out["qc"] = O_QC + cc; out["kc"] = O_KC + cc; out["vc"] = O_VC + cc; out["gc"] = O_GC + cc
    return out

def mixed_cols(c):
    ha, half = c // 2, c % 2
    return np.concatenate([ha * 256 + half * 128 + np.arange(128), 1024 + c * 128 + np.arange(128),
                           2048 + c * 256 + np.arange(256)])


D = 4096; KC = D // 128; NCOL = 2688
GROUPS = [(0, 896), (896, 768), (1664, 1024)]
_names = [("qa_eo",128),("qa_oe",128),("ka_eo",128),("ka_oe",128),("va",256),("ga",128),("rb",128),("kb",128),("vb",128),
          ("wd",128),("ad",128),("gb",128),("qc",256),("kc",256),("vc",256),("gc",256)]
OFF = {}; _o = 0
for _n, _w in _names: OFF[_n] = _o; _o += _w
assert _o == NCOL
QSCALE = 128 ** -0.5
SCAN_WAITS = False

class Ctx:
    def __init__(self, nc, P):
        self.nc, self.P = nc, P
        self.sems, self.psums, self.drams = {}, {}, {}
    def sem(self, name):
        if name not in self.sems: self.sems[name] = Sem(self.nc, name)
        return self.sems[name]
    def psum(self, name, shape, dt):
        if name not in self.psums: self.psums[name] = self.nc.psum_tensor(name, shape, dt).__enter__()
        return self.psums[name]
    def dram(self, name, shape, dt):
        if name not in self.drams: self.drams[name] = self.nc.dram_tensor(name, shape, dt)
        return self.drams[name]


def build_B(S, stage=1, ctx=None, io=None, tag=""):
    fused = ctx is not None
    nc = ctx.nc if fused else bass.Bass("TRN2", target_bir_lowering=False)
    NTT = S // 512
    def EXT(name, shape, dt, out=False):
        if fused: return io[name]
        return nc.dram_tensor(name, shape, dt, kind="ExternalOutput" if out else "ExternalInput")
    def SCR(name, shape, dt):
        return ctx.dram("B_" + name, shape, dt) if fused else nc.dram_tensor(name, shape, dt, kind="Internal")
    if fused:
        wc = io["wc"]; xnT_tile = io["xnT"]
    else:
        xnT = nc.dram_tensor("xnT", [D, S], BF16, kind="ExternalInput")
        wc = nc.dram_tensor("wc", [D, NCOL], F32, kind="ExternalInput")
        xnT_tile = lambda k, tt: xnT[k * 128:(k + 1) * 128, tt * 512:(tt + 1) * 512]
    hT = EXT("hT", [NCOL, S], F32, out=True) if (stage == 1 and not fused) else SCR("hT", [NCOL, S], F32)
    vtok = SCR("vtok", [S, 256], F32)
    vatok = SCR("vatok", [S, 256], F32)
    gtok = SCR("gtok", [S, 512], F32)
    ybtok = SCR("ybtok", [S, 128], F32)
    if stage >= 5:
        mixed = SCR("mixed", [S, 512], F32) if fused else EXT("mixed", [S, 512], F32, out=True)
    if stage >= 4:
        bpd = EXT("bp", [128, 12], F32)
        wupd = EXT("wup", [128, 128], F32)
        aupd = EXT("aup", [128, 128], F32)
        BDd = EXT("BD", [128, 128], F32)
        idBd = EXT("identB", [128, 128], F32)
        ybT = EXT("ybT", [128, S], F32, out=True) if (stage == 4 and not fused) else SCR("ybT", [128, S], F32)
    if stage >= 3:
        CSd = EXT("CS", [128, S], F32); SSd = EXT("SS", [128, S], F32)
        FSd = EXT("FS", [128, 128], F32)
        M2d = EXT("M2", [128, 128], F32)
        tgd = EXT("tg", [128, 2], F32)
        idd = EXT("identA", [128, 128], F32)
        ya = EXT("ya", [S, 256], F32, out=True) if (stage == 3 and not fused) else SCR("ya", [S, 256], F32)
    if stage >= 2:
        biasT = EXT("biasT", [128, 2, 5, 128], F32)
        yc = EXT("yc", [S, 256], F32, out=True) if (stage == 2 and not fused) else SCR("yc", [S, 256], F32)
    P = ctx.P if fused else Prog(nc)
    Sm = (lambda n: ctx.sem("B_" + n)) if fused else (lambda n: Sem(nc, n))
    _phase = []
    def A(name, shape, dt):
        cm = nc.sbuf_tensor(name + tag, shape, dt); t_ = cm.__enter__(); _phase.append(cm); return t_
    def end_phase():
        while _phase:
            _phase.pop().__exit__(None, None, None)
    def phase_barrier(waitlist):
        for q_ in ("pe", "act", "dve", "pool", "sp"):
            P.op(q_, lambda e: e.nop(), waits=list(waitlist))
    if fused:
        banks = [ctx.psum(f"bank{i}", [128, 512], F32) for i in range(8)]
        _pmap = {"ps0": banks[0], "ps1": banks[1]}
        Pm = lambda name, shape, dt: _pmap[name]
    else:
        Pm = lambda name, shape, dt: nc.psum_tensor(name, shape, dt).__enter__()
    if fused:
        phase_barrier(io["start"])
    wres = A("wres", [128, KC, 1024], BF16)
    xt = [A(f"xt{i}", [128, KC, 512], BF16) for i in range(2)]
    ot = [A(f"ot{i}", [128, 512], F32) for i in range(2)]
    ps = [Pm(f"ps{i}", [128, 512], F32) for i in range(2)]
    s_w = [Sm(f"w{g}") for g in range(len(GROUPS))]
    s_x = [Sm("x0"), Sm("x1")]; s_o = [Sm("o0"), Sm("o1")]
    s_mm = Sm("mm"); s_ev = Sm("ev")
    ev_vals = []; mm_last_of_xslot = [0, 0]; o_last = [0, 0]; mm_last_group = 0
    xi = 0; it = 0
    for g, (c0, ncol) in enumerate(GROUPS):
        for k in range(KC):
            v_w = P.op("pool", lambda e, k=k, c0=c0, ncol=ncol: e.dma_start(
                out=wres[:, k, 0:ncol], in_=wc[k * 128:(k + 1) * 128, c0:c0 + ncol]),
                waits=[(s_mm, mm_last_group)] if (k == 0 and mm_last_group) else [], inc=s_w[g], by=16)
        for tt in range(NTT):
            xb = xi % 2
            for k in range(KC):
                v_x = P.op("sp", lambda e, k=k, tt=tt, xb=xb: e.dma_start(
                    out=xt[xb][:, k, :], in_=xnT_tile(k, tt)),
                    waits=[(s_mm, mm_last_of_xslot[xb])] if (k == 0 and mm_last_of_xslot[xb]) else [],
                    inc=s_x[xb], by=16)
            for cb in range(ncol // 128):
                _c = c0 + cb * 128
                if stage >= 5 and any(OFF[nm_] <= _c < OFF[nm_] + w_ for nm_, w_ in (("va", 256), ("ga", 128), ("gb", 128), ("vc", 256), ("gc", 256))):
                    continue
                pb = it % 2
                for k in range(KC):
                    wts = []
                    if k == 0:
                        wts = [(s_w[g], v_w), (s_x[xb], v_x)]
                        if it >= 2:
                            wts.append((s_ev, ev_vals[it - 2]))
                    v_mm = P.op("pe", lambda e, pb=pb, k=k, cb=cb, xb=xb: e.matmul(
                        ps[pb][:], lhsT=wres[:, k, cb * 128:(cb + 1) * 128], rhs=xt[xb][:, k, :],
                        start=(k == 0), stop=(k == KC - 1)), waits=wts, inc=s_mm)
                wts = [(s_mm, v_mm)]
                if o_last[pb]:
                    wts.append((s_o[pb], o_last[pb]))
                sc_ = QSCALE if (OFF["qc"] <= c0 + cb * 128 < OFF["qc"] + 256 or OFF["ka_eo"] <= c0 + cb * 128 < OFF["ka_oe"] + 128) else 1.0
                v_ev = P.op("act", lambda e, pb=pb, sc_=sc_: e.activation(out=ot[pb][:], in_=ps[pb][:], func=AF.Copy, scale=sc_),
                            waits=wts, inc=s_ev)
                ev_vals.append(v_ev)
                r0 = c0 + cb * 128
                o_last[pb] = P.op("sp", lambda e, pb=pb, r0=r0, tt=tt: e.dma_start(
                    out=hT[r0:r0 + 128, tt * 512:(tt + 1) * 512], in_=ot[pb][:]),
                    waits=[(s_ev, v_ev)], inc=s_o[pb], by=16)
                it += 1
            tm_jobs = []
            if g == 0 and stage >= 3: tm_jobs.append((OFF["va"] - c0, 256, vatok, 0, AF.Copy))
            if g == 2 and stage >= 2: tm_jobs.append((OFF["vc"] - c0, 256, vtok, 0, AF.Copy))
            if stage >= 5:
                if g == 0: tm_jobs.append((OFF["ga"] - c0, 128, gtok, 0, AF.Silu))
                if g == 1: tm_jobs.append((OFF["gb"] - c0, 128, gtok, 128, AF.Silu))
                if g == 2: tm_jobs.append((OFF["gc"] - c0, 256, gtok, 256, AF.Silu))
            for (jc0, jw, jdst, jd0, jfn) in tm_jobs:
                for st4 in range(4):
                    pb = it % 2
                    for k in range(KC):
                        wts = []
                        if k == 0 and it >= 2:
                            wts.append((s_ev, ev_vals[it - 2]))
                        v_mm = P.op("pe", lambda e, pb=pb, k=k, xb=xb, st4=st4, jc0=jc0, jw=jw: e.matmul(
                            ps[pb][:, 0:jw], lhsT=xt[xb][:, k, st4 * 128:(st4 + 1) * 128], rhs=wres[:, k, jc0:jc0 + jw],
                            start=(k == 0), stop=(k == KC - 1)), waits=wts, inc=s_mm)
                    wts = [(s_mm, v_mm)]
                    if o_last[pb]:
                        wts.append((s_o[pb], o_last[pb]))
                    v_ev = P.op("act", lambda e, pb=pb, jw=jw, jfn=jfn: e.activation(out=ot[pb][:, 0:jw], in_=ps[pb][:, 0:jw], func=jfn),
                                waits=wts, inc=s_ev)
                    ev_vals.append(v_ev)
                    t0_ = tt * 512 + st4 * 128
                    o_last[pb] = P.op("sp", lambda e, pb=pb, t0_=t0_, jdst=jdst, jd0=jd0, jw=jw: e.dma_start(
                        out=jdst[t0_:t0_ + 128, jd0:jd0 + jw], in_=ot[pb][:, 0:jw]), waits=[(s_ev, v_ev)], inc=s_o[pb], by=16)
                    it += 1
            mm_last_of_xslot[xb] = v_mm
            xi += 1
        mm_last_group = v_mm

    end_phase()
    if stage >= 2:
        NQ = S // 128
        v_proj_done = [(s_o[0], s_o[0].n), (s_o[1], s_o[1].n)]
        v_proj_done += [(s_mm, s_mm.n), (s_ev, s_ev.n)]
        phase_barrier(v_proj_done)
        qT = A("qT", [128, 2, S], BF16); kT = A("kT", [128, 2, S], BF16)
        V1 = A("V1", [128, NQ, 2, 129], BF16); bT = A("bT", [128, 2, 5, 128], F32)
        s_cl = Sm("cl"); s_one = Sm("one")
        v_one = P.op("dve", lambda e: e.memset(V1[:, :, :, 128:129], 1.0), inc=s_one)
        for h in range(2):
            for c0_ in range(0, S, 2048):
                c1_ = min(S, c0_ + 2048)
                P.op("pool", lambda e, h=h, c0_=c0_, c1_=c1_: e.dma_start(out=qT[:, h, c0_:c1_], in_=hT[OFF["qc"] + h * 128:OFF["qc"] + (h + 1) * 128, c0_:c1_]),
                     waits=v_proj_done if (h == 0 and c0_ == 0) else [], inc=s_cl, by=16)
                P.op("pool", lambda e, h=h, c0_=c0_, c1_=c1_: e.dma_start(out=kT[:, h, c0_:c1_], in_=hT[OFF["kc"] + h * 128:OFF["kc"] + (h + 1) * 128, c0_:c1_]),
                     inc=s_cl, by=16)
        for j in range(NQ):
            P.op("pool", lambda e, j=j: e.dma_start(out=V1[:, j, :, 0:128],
                 in_=vtok[j * 128:(j + 1) * 128, :].rearrange("p (h d) -> p h d", h=2)),
                 waits=[(s_one, v_one)] if j == 0 else [], inc=s_cl, by=16)
        P.op("sp", lambda e: e.dma_start(out=bT[:], in_=biasT[:, :, :, :]), inc=s_cl, by=16)
        v_cl = s_cl.n
        if fused:
            c_psum_cms = []
            scA = [banks[2][:, :].rearrange("p (a b) -> p a b", a=4), banks[3][:, :].rearrange("p (a b) -> p a b", a=4)]
            scB = [banks[4][:, 0:128].rearrange("p (a b) -> p a b", a=1), banks[5][:, 0:128].rearrange("p (a b) -> p a b", a=1)]
            oP = [banks[6][:, 0:129], banks[7][:, 0:129]]
        else:
            c_psum_cms = [nc.psum_tensor(nm, shp, F32) for nm, shp in
                          [("scA0", [128, 4, 128]), ("scA1", [128, 4, 128]), ("scB0", [128, 1, 128]), ("scB1", [128, 1, 128]),
                           ("oP0", [128, 129]), ("oP1", [128, 129])]]
            _t = [cm.__enter__() for cm in c_psum_cms]
            scA, scB, oP = _t[0:2], _t[2:4], _t[4:6]
        sb = [A(f"sb{i}", [128, 5, 128], F32) for i in range(2)]
        pT = [A(f"pT{i}", [128, 5, 128], BF16) for i in range(2)]
        rc = [A(f"rc{i}", [128, 1], F32) for i in range(2)]
        yt = [A(f"yt{i}", [128, 128], F32) for i in range(2)]
        s_sc = Sm("sc"); s_ba = Sm("ba"); s_ex = Sm("ex"); s_pv = Sm("pv"); s_rc = Sm("rc"); s_y = Sm("y")
        s_ys = [Sm("ys0"), Sm("ys1")]
        ba_v, ex_v, pv_v, y_v, ys_v = [], [], [], [], [0, 0]
        n = 0
        for h in range(2):
            for i in range(NQ):
                b = n % 2
                slots = [(sl, i - 4 + sl) for sl in range(5) if i - 4 + sl >= 0]
                sA = [x for x in slots if x[0] < 4]; sBl = [x for x in slots if x[0] == 4]
                first = True
                for (sl, jt) in slots:
                    dst = scA[b][:, sl, :] if sl < 4 else scB[b][:, 0, :]
                    wts = []
                    if first:
                        wts = [(s_cl, v_cl)] if n == 0 else []
                        if n >= 2: wts.append((s_ba, ba_v[n - 2]))
                        first = False
                    v_sc = P.op("pe", lambda e, dst=dst, h=h, jt=jt, i=i: e.matmul(
                        dst, lhsT=kT[:, h, jt * 128:(jt + 1) * 128], rhs=qT[:, h, i * 128:(i + 1) * 128],
                        start=True, stop=True), waits=wts, inc=s_sc)
                lo = sA[0][0] if sA else 4
                wts = [(s_sc, v_sc)] + ([(s_ex, ex_v[n - 2])] if n >= 2 else [])
                if sA:
                    v_ba = P.op("dve", lambda e, b=b, lo=lo, h=h: e.tensor_tensor(
                        out=sb[b][:, lo:4, :], in0=scA[b][:, lo:4, :], in1=bT[:, h, lo:4, :], op=ALU.add), waits=wts, inc=s_ba)
                    wts = []
                v_ba = P.op("dve", lambda e, b=b, h=h: e.tensor_tensor(
                    out=sb[b][:, 4:5, :], in0=scB[b][:, 0:1, :], in1=bT[:, h, 4:5, :], op=ALU.add), waits=wts, inc=s_ba)
                ba_v.append(v_ba)
                wts = [(s_ba, v_ba)] + ([(s_pv, pv_v[n - 2])] if n >= 2 else [])
                v_ex = P.op("act", lambda e, b=b, lo=lo: e.activation(out=pT[b][:, lo:5, :], in_=sb[b][:, lo:5, :], func=AF.Exp),
                            waits=wts, inc=s_ex)
                ex_v.append(v_ex)
                for q_, (sl, jt) in enumerate(slots):
                    wts = []
                    if q_ == 0:
                        wts = [(s_ex, v_ex)] + ([(s_y, y_v[n - 2])] if n >= 2 else [])
                    v_pv = P.op("pe", lambda e, b=b, sl=sl, jt=jt, h=h, q_=q_, L=len(slots): e.matmul(
                        oP[b][:], lhsT=pT[b][:, sl, :], rhs=V1[:, jt, h, :], start=(q_ == 0), stop=(q_ == L - 1)),
                        waits=wts, inc=s_pv)
                pv_v.append(v_pv)
                v_rc = P.op("dve", lambda e, b=b: e.reciprocal(out=rc[b][:], in_=oP[b][:, 128:129]),
                            waits=[(s_pv, v_pv)] + ([(s_y, y_v[n - 2])] if n >= 2 else []), inc=s_rc)
                wts = [(s_rc, v_rc)] + ([(s_ys[b], ys_v[b])] if ys_v[b] else [])
                v_y = P.op("dve", lambda e, b=b: e.tensor_scalar(out=yt[b][:], in0=oP[b][:, 0:128], scalar1=rc[b][:, 0:1],
                                                                 scalar2=None, op0=ALU.mult), waits=wts, inc=s_y)
                y_v.append(v_y)
                ys_v[b] = P.op("sp", lambda e, b=b, i=i, h=h: e.dma_start(
                    out=yc[i * 128:(i + 1) * 128, h * 128:(h + 1) * 128], in_=yt[b][:]), waits=[(s_y, v_y)], inc=s_ys[b], by=16)
                n += 1
        P.op("sp", lambda e: e.nop(), waits=[(s_ys[0], s_ys[0].n), (s_ys[1], s_ys[1].n)])

    end_phase()
    if stage >= 3:
        NQ = S // 128
        a_start = [(s_o[0], s_o[0].n), (s_o[1], s_o[1].n)]
        if stage >= 2:
            a_start += [(s_ys[0], s_ys[0].n), (s_ys[1], s_ys[1].n), (s_pv, s_pv.n), (s_y, s_y.n)]
            a_start += [(s_sc, s_sc.n), (s_ba, s_ba.n), (s_ex, s_ex.n), (s_rc, s_rc.n), (s_cl, s_cl.n), (s_one, s_one.n)]
        phase_barrier(a_start)
        qr = A("qr", [128, S], BF16); kr = A("kr", [128, S], BF16); qs = A("qs", [128, S], BF16)
        Va = A("Va", [128, NQ, 256], BF16)
        M2 = A("M2s", [128, 128], F32); FS = A("FSs", [128, 128], F32); tg = A("tgs", [128, 2], F32)
        idf = A("idAf", [128, 128], F32); idb = A("idAb", [128, 128], BF16)
        s_ac = Sm("ac")
        for dst, src_ in ((M2, M2d), (FS, FSd), (tg, tgd), (idf, idd)):
            P.op("sp", lambda e, dst=dst, src_=src_: e.dma_start(out=dst[:], in_=src_[:, :]), waits=a_start if dst is M2 else [], inc=s_ac, by=16)
        for j in range(NQ):
            P.op("pool", lambda e, j=j: e.dma_start(out=Va[:, j, :], in_=vatok[j * 128:(j + 1) * 128, :]), inc=s_ac, by=16)
        v_ac = s_ac.n
        s_idb = Sm("idb")
        v_idb = P.op("dve", lambda e: e.tensor_copy(out=idb[:], in_=idf[:]), waits=[(s_ac, v_ac)], inc=s_idb)
        ta = A("ropeA", [128, 512], F32); tb = A("ropeB", [128, 512], F32); tcs = A("ropeC", [128, 512], F32); tss = A("ropeS", [128, 512], F32)
        t1 = A("ropeT1", [128, 512], F32); t2 = A("ropeT2", [128, 512], F32)
        s_rl = Sm("arl"); s_rd = Sm("ard")
        v_rd = 0
        for which, (ra, rb_, dstT) in enumerate((("qa_eo", "qa_oe", qr), ("ka_eo", "ka_oe", kr))):
            for tt in range(S // 512):
                cs_ = slice(tt * 512, (tt + 1) * 512)
                for dst, src_, r0 in ((ta, hT, OFF[ra]), (tb, hT, OFF[rb_]), (tcs, CSd, 0), (tss, SSd, 0)):
                    v_l = P.op("sp", lambda e, dst=dst, src_=src_, r0=r0, cs_=cs_: e.dma_start(out=dst[:], in_=src_[r0:r0 + 128, cs_]),
                               waits=([(s_rd, v_rd)] if v_rd else a_start) if dst is ta else [], inc=s_rl, by=16)
                v1 = P.op("dve", lambda e: e.tensor_tensor(out=t1[:], in0=ta[:], in1=tcs[:], op=ALU.mult), waits=[(s_rl, v_l)], inc=s_rd)
                v2 = P.op("dve", lambda e: e.tensor_tensor(out=t2[:], in0=tb[:], in1=tss[:], op=ALU.mult), waits=[(s_rd, v1)], inc=s_rd)
                v_rd = P.op("dve", lambda e, dstT=dstT, cs_=cs_: e.tensor_tensor(out=dstT[:, cs_], in0=t1[:], in1=t2[:], op=ALU.add),
                            waits=[(s_rd, v2)], inc=s_rd)
                if which == 0:
                    for u in range(4):
                        v_rd = P.op("dve", lambda e, tt=tt, u=u: e.tensor_tensor(
                            out=qs[:, tt * 512 + u * 128: tt * 512 + (u + 1) * 128], in0=qr[:, tt * 512 + u * 128: tt * 512 + (u + 1) * 128],
                            in1=FS[:], op=ALU.mult), waits=[(s_rd, v_rd), (s_ac, v_ac)], inc=s_rd)
        v_rope = v_rd
        Sf = A("Sf", [128, 256], F32); Sb = A("Sb", [128, 256], BF16)
        pTa = A("pTa", [128, 128], BF16); kte = A("kte", [128, 128], BF16)
        sqj = A("sqj", [128, 256], F32); ssA = A("ssA", [128, 1], F32); rsA = A("rsA", [128, 1], F32); yA = A("yA", [128, 256], F32)
        tpv = ps[1][:, 384:448].bitcast(BF16)
        s_a = Sm("a")
        s_ayst = Sm("ayst")
        v = P.op("dve", lambda e: e.memset(Sf[:], 0.0), waits=[(s_rd, v_rope)], inc=s_a)
        v = P.op("dve", lambda e: e.memset(Sb[:], 0.0), waits=[(s_a, v)], inc=s_a)
        v_st = 0
        for i in range(NQ):
            cs_ = slice(i * 128, (i + 1) * 128)
            v = P.op("pe", lambda e, cs_=cs_: e.matmul(ps[1][:, 0:128], lhsT=kr[:, cs_], rhs=qr[:, cs_], start=True, stop=True),
                     waits=[(s_a, v), (s_idb, v_idb)], inc=s_a)
            v = P.op("dve", lambda e: e.tensor_tensor(out=pTa[:], in0=ps[1][:, 0:128], in1=M2[:], op=ALU.mult), waits=[(s_a, v)], inc=s_a)
            v = P.op("pe", lambda e, i=i: e.matmul(ps[0][:, 0:256], lhsT=pTa[:], rhs=Va[:, i, :], start=True, stop=False), waits=[(s_a, v)], inc=s_a)
            v = P.op("pe", lambda e, cs_=cs_: e.matmul(ps[0][:, 0:256], lhsT=qs[:, cs_], rhs=Sb[:], start=False, stop=True), waits=[(s_a, v)], inc=s_a)
            v = P.op("act", lambda e: e.activation(out=sqj[:], in_=ps[0][:, 0:256], func=AF.Square, accum_out=ssA[:]), waits=[(s_a, v)], inc=s_a)
            v = P.op("dve", lambda e: e.tensor_scalar(out=rsA[:], in0=ssA[:], scalar1=1.0 / 256, scalar2=1e-6, op0=ALU.mult, op1=ALU.add), waits=[(s_a, v)], inc=s_a)
            v = P.op("act", lambda e: e.activation(out=rsA[:], in_=rsA[:], func=AF.Sqrt), waits=[(s_a, v)], inc=s_a)
            v = P.op("dve", lambda e: e.reciprocal(out=rsA[:], in_=rsA[:]), waits=[(s_a, v)], inc=s_a)
            v = P.op("dve", lambda e: e.tensor_scalar(out=yA[:], in0=ps[0][:, 0:256], scalar1=rsA[:, 0:1], scalar2=None, op0=ALU.mult),
                     waits=[(s_a, v)] + ([(s_ayst, v_st)] if v_st else []), inc=s_a)
            v_st = P.op("sp", lambda e, cs_=cs_: e.dma_start(out=ya[cs_, :], in_=yA[:]), waits=[(s_a, v)], inc=s_ayst, by=16)
            v = P.op("pe", lambda e, cs_=cs_: e.transpose(out=tpv, in_=kr[:, cs_], identity=idb[:]), waits=[(s_a, v)], inc=s_a)
            v = P.op("act", lambda e: e.activation(out=kte[:], in_=tpv, func=AF.Copy, scale=tg[:, 0:1]), waits=[(s_a, v)], inc=s_a)
            v = P.op("pe", lambda e, i=i: e.matmul(ps[1][:, 128:384], lhsT=kte[:], rhs=Va[:, i, :], start=True, stop=True), waits=[(s_a, v)], inc=s_a)
            v = P.op("dve", lambda e: e.scalar_tensor_tensor(out=Sf[:], in0=Sf[:], scalar=tg[:, 1:2], in1=ps[1][:, 128:384], op0=ALU.mult, op1=ALU.add),
                     waits=[(s_a, v)], inc=s_a)
            v = P.op("act", lambda e: e.activation(out=Sb[:], in_=Sf[:], func=AF.Copy), waits=[(s_a, v)], inc=s_a)
        P.op("sp", lambda e: e.nop(), waits=[(s_ayst, s_ayst.n), (s_a, v)])

    end_phase()
    if stage >= 4:
        b_start = [(s_o[0], s_o[0].n), (s_o[1], s_o[1].n)]
        if stage >= 2 and "s_pv" in dir():
            pass
        b_start += [(s_ys[0], s_ys[0].n), (s_ys[1], s_ys[1].n), (s_pv, s_pv.n), (s_y, s_y.n), (s_ayst, s_ayst.n), (s_a, s_a.n)]
        b_start += [(s_rd, s_rd.n), (s_ac, s_ac.n), (s_idb, s_idb.n), (s_rl, s_rl.n)]
        phase_barrier(b_start)
        for cm in reversed(c_psum_cms):
            cm.__exit__(None, None, None)
        if fused:
            bc = [banks[2 + i][:, 0:320].rearrange("p (k j) -> p k j", k=5) for i in range(4)]
        else:
            bc = [Pm(f"bc{i}", [128, 5, 64], F32) for i in range(4)]
        bp = A("bps", [128, 12], F32); wup = A("wups", [128, 128], F32); aup = A("aups", [128, 128], F32)
        BD = A("BDs", [128, 128], F32); idB = A("idB", [128, 128], F32)
        s_bl = Sm("bl")
        for dst, src_ in ((bp, bpd), (wup, wupd), (aup, aupd), (BD, BDd), (idB, idBd)):
            P.op("sp", lambda e, dst=dst, src_=src_: e.dma_start(out=dst[:], in_=src_[:, :]), waits=b_start if dst is bp else [], inc=s_bl, by=16)
        v_bl = s_bl.n
        TW = min(512, S); NSUB = TW // 128
        T = lambda nm, w=TW: A(nm, [128, w], F32)
        zin = {k: T("z_" + k, TW + 1) for k in ("r", "k", "v", "wd", "ad")}
        xs_ = {k: T("x_" + k) for k in ("r", "k", "v", "wd", "ad")}
        dtmp = T("dtmp"); wt = T("w_t"); at = T("a_t"); kkr = T("kkr"); sqt = T("sqt"); rn = T("rn"); kk = T("kk"); nkk = T("nkk"); bt = T("b_t")
        km = T("km"); rk = T("rk"); bon = T("bon")
        Xtok = A("Xtok", [128, NSUB, 5, 128], F32)
        St = A("St", [128, 64], F32); sa = A("sa", [128, 1], F32); ycol = T("ycol"); junk = A("junk", [128, 64], F32)
        dln = T("dln"); sqd = T("sqd"); rstd = T("rstd"); yo_ = T("yo_")
        s_zl = Sm("zl"); s_p = Sm("p"); s_bc = Sm("bcs"); s_d = Sm("d"); s_yst = Sm("yst")
        ytk = A("ytk", [128, NSUB, 128], F32); s_ytk = Sm("ytk"); v_ytk = 0
        rows = {"r": OFF["rb"], "k": OFF["kb"], "v": OFF["vb"], "wd": OFF["wd"], "ad": OFF["ad"]}
        mucol = {"r": 0, "k": 1, "v": 2, "wd": 3, "ad": 4}
        chain = [0]
        def step(eng, fn, extra=()):
            w = ([(s_p, chain[0])] if chain[0] else []) + list(extra)
            chain[0] = P.op(eng, fn, waits=w, inc=s_p)
            return chain[0]
        step("dve", lambda e: e.memset(St[:], 0.0), extra=[(s_bl, v_bl)])
        nstep = 0; v_yst = 0; d_hist = []
        for n in range(S // TW):
            t0 = n * TW
            for kname, z in zin.items():
                r0 = rows[kname]
                if n == 0:
                    step("dve", lambda e, z=z: e.memset(z[:, 0:1], 0.0))
                    v_l = P.op("sp", lambda e, z=z, r0=r0: e.dma_start(out=z[:, 1:TW + 1], in_=hT[r0:r0 + 128, 0:TW]),
                               waits=[(s_p, chain[0])], inc=s_zl, by=16)
                else:
                    v_l = P.op("sp", lambda e, z=z, r0=r0, t0=t0: e.dma_start(out=z[:, 0:TW + 1], in_=hT[r0:r0 + 128, t0 - 1:t0 + TW]),
                               waits=[(s_p, chain[0])], inc=s_zl, by=16)
            first = True
            for kname, z in zin.items():
                step("dve", lambda e, z=z: e.tensor_tensor(out=dtmp[:], in0=z[:, 0:TW], in1=z[:, 1:TW + 1], op=ALU.subtract),
                     extra=[(s_zl, v_l)] if first else []); first = False
                step("dve", lambda e, z=z, kname=kname: e.scalar_tensor_tensor(out=xs_[kname][:], in0=dtmp[:], scalar=bp[:, mucol[kname]:mucol[kname] + 1],
                                                                               in1=z[:, 1:TW + 1], op0=ALU.mult, op1=ALU.add))
            step("act", lambda e: e.activation(out=dtmp[:], in_=xs_["wd"][:], func=AF.Tanh))
            step("pe", lambda e: e.matmul(ps[0][:, 0:TW], lhsT=wup[:], rhs=dtmp[:], start=True, stop=True))
            step("act", lambda e: e.activation(out=wt[:], in_=ps[0][:, 0:TW], func=AF.Sigmoid, bias=bp[:, 5:6]))
            step("act", lambda e: e.activation(out=wt[:], in_=wt[:], func=AF.Exp, scale=-0.6065306597126334))
            step("pe", lambda e: e.matmul(ps[1][:, 0:TW], lhsT=aup[:], rhs=xs_["ad"][:], start=True, stop=True))
            step("act", lambda e: e.activation(out=at[:], in_=ps[1][:, 0:TW], func=AF.Sigmoid, bias=bp[:, 6:7]))
            step("dve", lambda e: e.tensor_scalar(out=kkr[:], in0=xs_["k"][:], scalar1=bp[:, 7:8], scalar2=None, op0=ALU.mult))
            step("dve", lambda e: e.tensor_tensor(out=sqt[:], in0=kkr[:], in1=kkr[:], op=ALU.mult))
            step("pe", lambda e: e.matmul(ps[0][:, 0:TW], lhsT=BD[:], rhs=sqt[:], start=True, stop=True))
            step("act", lambda e: e.activation(out=rn[:], in_=ps[0][:, 0:TW], func=AF.Sqrt))
            step("dve", lambda e: e.tensor_scalar(out=rn[:], in0=rn[:], scalar1=1e-12, scalar2=None, op0=ALU.max))
            step("dve", lambda e: e.reciprocal(out=rn[:], in_=rn[:]))
            step("dve", lambda e: e.tensor_tensor(out=kk[:], in0=kkr[:], in1=rn[:], op=ALU.mult))
            step("dve", lambda e: e.tensor_scalar(out=nkk[:], in0=kk[:], scalar1=-1.0, scalar2=None, op0=ALU.mult))
            step("dve", lambda e: e.tensor_tensor(out=bt[:], in0=kk[:], in1=at[:], op=ALU.mult))
            step("dve", lambda e: e.tensor_scalar(out=km[:], in0=at[:], scalar1=-1.0, scalar2=bp[:, 8:9], op0=ALU.add, op1=ALU.mult))
            step("dve", lambda e: e.scalar_tensor_tensor(out=km[:], in0=km[:], scalar=1.0, in1=xs_["k"][:], op0=ALU.add, op1=ALU.mult))
            step("dve", lambda e: e.scalar_tensor_tensor(out=rk[:], in0=xs_["r"][:], scalar=bp[:, 9:10], in1=km[:], op0=ALU.mult, op1=ALU.mult))
            step("pe", lambda e: e.matmul(ps[1][:, 0:TW], lhsT=BD[:], rhs=rk[:], start=True, stop=True))
            step("dve", lambda e: e.tensor_tensor(out=bon[:], in0=ps[1][:, 0:TW], in1=xs_["v"][:], op=ALU.mult))
            for kind, src_t in enumerate((nkk, wt, bt, km, xs_["r"])):
                pbk = ps[kind % 2]
                for sub in range(NSUB):
                    step("pe", lambda e, src_t=src_t, sub=sub, pbk=pbk: e.transpose(out=pbk[:, sub * 128:(sub + 1) * 128], in_=src_t[:, sub * 128:(sub + 1) * 128], identity=idB[:]))
                step("act", lambda e, kind=kind, pbk=pbk: e.activation(out=Xtok[:, :, kind, :], in_=pbk[:, 0:TW].rearrange("p (s c) -> p s c", s=NSUB), func=AF.Copy))
            v_prep = chain[0]
            for tt in range(TW):
                sub, t = tt // 128, tt % 128
                sl = nstep % 4
                sel = idB[:, t:t + 1].broadcast_to([128, 64])
                wts = [(s_p, v_prep)] if tt == 0 else []
                if nstep >= 4:
                    wts.append((s_d, d_hist[nstep - 4]))
                P.op("pe", lambda e, sl=sl, sel=sel, sub=sub: e.matmul(bc[sl][0:64, :, :], lhsT=sel, rhs=Xtok[:, sub, :, 0:64], start=True, stop=True), waits=wts, inc=s_bc)
                v_b = P.op("pe", lambda e, sl=sl, sel=sel, sub=sub: e.matmul(bc[sl][64:128, :, :], lhsT=sel, rhs=Xtok[:, sub, :, 64:128], start=True, stop=True), inc=s_bc)
                B_ = bc[sl]
                W = (lambda v_: [(s_d, v_)]) if SCAN_WAITS else (lambda v_: [])
                w_first = [(s_bc, v_b)] + ([(s_p, v_prep)] if tt == 0 else [])
                if SCAN_WAITS and s_d.n:
                    w_first.append((s_d, s_d.n))
                I = s_d if SCAN_WAITS else None
                v = P.op("dve", lambda e, B_=B_: e.scalar_tensor_tensor(out=junk[:], in0=St[:], scalar=1.0, in1=B_[:, 0, :],
                                                                        op0=ALU.mult, op1=ALU.mult, accum_out=sa[:]), waits=w_first, inc=I)
                v = P.op("dve", lambda e, B_=B_: e.tensor_tensor(out=St[:], in0=St[:], in1=B_[:, 1, :], op=ALU.mult), waits=W(v), inc=I)
                v = P.op("dve", lambda e, B_=B_: e.scalar_tensor_tensor(out=St[:], in0=B_[:, 2, :], scalar=sa[:, 0:1], in1=St[:], op0=ALU.mult, op1=ALU.add),
                         waits=W(v), inc=I)
                v = P.op("dve", lambda e, B_=B_, tt=tt: e.scalar_tensor_tensor(out=St[:], in0=B_[:, 3, :], scalar=xs_["v"][:, tt:tt + 1], in1=St[:], op0=ALU.mult, op1=ALU.add),
                         waits=W(v), inc=I)
                v = P.op("dve", lambda e, B_=B_, tt=tt: e.scalar_tensor_tensor(out=junk[:], in0=St[:], scalar=1.0, in1=B_[:, 4, :],
                                                                               op0=ALU.mult, op1=ALU.mult, accum_out=ycol[:, tt:tt + 1]),
                         waits=W(v), inc=s_d)
                d_hist.append(v); nstep += 1
            step("pe", lambda e: e.matmul(ps[0][:, 0:TW], lhsT=BD[:], rhs=ycol[:], start=True, stop=True), extra=[(s_d, d_hist[-1])])
            step("dve", lambda e: e.scalar_tensor_tensor(out=dln[:], in0=ps[0][:, 0:TW], scalar=-1.0 / 64, in1=ycol[:], op0=ALU.mult, op1=ALU.add))
            step("dve", lambda e: e.tensor_tensor(out=sqd[:], in0=dln[:], in1=dln[:], op=ALU.mult))
            step("pe", lambda e: e.matmul(ps[1][:, 0:TW], lhsT=BD[:], rhs=sqd[:], start=True, stop=True))
            step("dve", lambda e: e.tensor_scalar(out=rstd[:], in0=ps[1][:, 0:TW], scalar1=1.0 / 64, scalar2=64e-5, op0=ALU.mult, op1=ALU.add))
            step("act", lambda e: e.activation(out=rstd[:], in_=rstd[:], func=AF.Sqrt))
            step("dve", lambda e: e.reciprocal(out=rstd[:], in_=rstd[:]))
            step("dve", lambda e: e.tensor_tensor(out=dln[:], in0=dln[:], in1=rstd[:], op=ALU.mult))
            step("dve", lambda e: e.tensor_scalar(out=dln[:], in0=dln[:], scalar1=bp[:, 10:11], scalar2=bp[:, 11:12], op0=ALU.mult, op1=ALU.add))
            step("dve", lambda e: e.tensor_tensor(out=yo_[:], in0=dln[:], in1=bon[:], op=ALU.add), extra=[(s_yst, v_yst)] if v_yst else [])
            v_yst = P.op("sp", lambda e, t0=t0: e.dma_start(out=ybT[:, t0:t0 + TW], in_=yo_[:]), waits=[(s_p, chain[0])], inc=s_yst, by=16)
            if stage >= 5:
                for sub in range(NSUB):
                    step("pe", lambda e, sub=sub: e.transpose(out=ps[0][:, sub * 128:(sub + 1) * 128], in_=yo_[:, sub * 128:(sub + 1) * 128], identity=idB[:]))
                step("act", lambda e: e.activation(out=ytk[:], in_=ps[0][:, 0:TW].rearrange("p (s c) -> p s c", s=NSUB), func=AF.Copy),
                     extra=[(s_ytk, v_ytk)] if v_ytk else [])
                v_ytk = P.op("sp", lambda e, t0=t0: e.dma_start(out=ybtok[t0:t0 + TW, :].rearrange("(s p) c -> p s c", p=128), in_=ytk[:]),
                             waits=[(s_p, chain[0])], inc=s_ytk, by=16)
        P.op("sp", lambda e: e.nop(), waits=[(s_yst, s_yst.n)])

    if stage >= 5:
        f_start = [(s_yst, s_yst.n), (s_ytk, s_ytk.n), (s_p, chain[0]), (s_d, s_d.n), (s_bc, s_bc.n), (s_zl, s_zl.n), (s_bl, s_bl.n)]
        end_phase()
        phase_barrier(f_start)
        GB = min(4, S // 128)
        ycat = [A(f"ycat{i}", [128, GB, 512], F32) for i in range(2)]
        gt = [A(f"gt{i}", [128, GB, 512], F32) for i in range(2)]
        mo = [A(f"mo{i}", [128, GB, 512], F32) for i in range(2)]
        s_fl = [Sm("fl0"), Sm("fl1")]; s_fm = Sm("fm"); s_fs = [Sm("fs0"), Sm("fs1")]
        fm_v, fs_v = [], [0, 0]
        tm = lambda ap_: ap_.rearrange("(s p) c -> p s c", p=128)
        for n in range(S // (128 * GB)):
            b = n % 2; r_ = slice(n * 128 * GB, (n + 1) * 128 * GB)
            w0_ = [(s_fm, fm_v[n - 2])] if n >= 2 else []
            P.op("sp", lambda e, b=b, r_=r_: e.dma_start(out=ycat[b][:, :, 0:128], in_=tm(ya[r_, 0:128])), waits=w0_, inc=s_fl[b], by=16)
            P.op("sp", lambda e, b=b, r_=r_: e.dma_start(out=ycat[b][:, :, 128:256], in_=tm(ybtok[r_, :])), inc=s_fl[b], by=16)
            P.op("sp", lambda e, b=b, r_=r_: e.dma_start(out=ycat[b][:, :, 256:512], in_=tm(yc[r_, :])), inc=s_fl[b], by=16)
            v_l = P.op("sp", lambda e, b=b, r_=r_: e.dma_start(out=gt[b][:], in_=tm(gtok[r_, :])), inc=s_fl[b], by=16)
            v_m = P.op("dve", lambda e, b=b: e.tensor_tensor(out=mo[b][:], in0=ycat[b][:], in1=gt[b][:], op=ALU.mult),
                       waits=[(s_fl[b], v_l)] + ([(s_fs[b], fs_v[b])] if fs_v[b] else []), inc=s_fm)
            fm_v.append(v_m)
            fs_v[b] = P.op("sp", lambda e, b=b, r_=r_: e.dma_start(out=tm(mixed[r_, :]), in_=mo[b][:]), waits=[(s_fm, v_m)], inc=s_fs[b], by=16)
        P.op("sp", lambda e: e.nop(), waits=[(s_fs[0], s_fs[0].n), (s_fs[1], s_fs[1].n)])
    if fused:
        io["mixed"] = mixed
        io["done"] = [(s_fs[0], s_fs[0].n), (s_fs[1], s_fs[1].n), (s_fm, s_fm.n), (s_fl[0], s_fl[0].n), (s_fl[1], s_fl[1].n)]
        end_phase()
        return None
    P.op("sp", lambda e: e.nop(), waits=[(s_o[0], s_o[0].n), (s_o[1], s_o[1].n)])
    P.emit()
    return nc


RG4 = [[0, 1, 2, 3], [4, 5, 6, 7]]
RG2X = [[i, i + 4] for i in range(4)]


def shard_of_core(c):
    return 2 * c if c < 4 else 2 * (c - 4) + 1


def build_F(S, depth):
    nc = bass.Bass("TRN2", target_bir_lowering=False)
    T = S // 8
    NT = T // 128
    P = Prog(nc)
    ctx = Ctx(nc, P)
    ext = lambda name, shape, dt: nc.dram_tensor(name, shape, dt, kind="ExternalInput")
    x_sh = ext("x", [T, D], F32)
    gN = ext("gN", [depth + 1, D], F32)
    wc = ext("wc", [depth, D, NCOL], F32)
    wo = ext("wo", [depth, 512, D], F32)
    biasT = ext("biasT", [depth, 128, 2, 5, 128], F32)
    bp = ext("bp", [depth, 128, 12], F32)
    wup = ext("wup", [depth, 128, 128], F32)
    aup = ext("aup", [depth, 128, 128], F32)
    CS = ext("CS", [128, S], F32); SS = ext("SS", [128, S], F32)
    FS = ext("FS", [128, 128], F32); M2 = ext("M2", [128, 128], F32); tg = ext("tg", [128, 2], F32)
    ident = ext("ident", [128, 128], F32); BD = ext("BD", [128, 128], F32)
    out = nc.dram_tensor("out", [T, D], F32, kind="ExternalOutput")
    CA, NCA = 256, D // 256
    CW, NCR = 128, D // 128
    xnT_sh = nc.dram_tensor("xnT_sh", [D, T], BF16)
    ag2 = nc.dram_tensor("ag2", [NCA * 2 * CA, T], BF16)
    xnT_g = nc.dram_tensor("xnT_g", [NCA * 8 * CA, T], BF16)
    partial = nc.dram_tensor("partial", [NCR * S, CW], F32)
    rs4 = nc.dram_tensor("rs4", [NCR * (S // 4), CW], F32)
    delta = nc.dram_tensor("delta", [NCR * T, CW], F32)
    xcur = nc.dram_tensor("xcur", [T, D], F32, kind="Internal")

    banks = [ctx.psum(f"bank{i}", [128, 512], F32) for i in range(8)]
    Sm = lambda n: ctx.sem("F_" + n)
    _phase = []

    def A(name, shape, dt):
        cm = nc.sbuf_tensor(name, shape, dt); t_ = cm.__enter__(); _phase.append(cm); return t_

    def end_phase():
        while _phase:
            _phase.pop().__exit__(None, None, None)

    def phase_barrier(waitlist):
        for q_ in ("pe", "act", "dve", "pool", "sp"):
            P.op(q_, lambda e: e.nop(), waits=list(waitlist))

    s_cc = Sm("cc")

    def norm_phase(l, start):
        final = l == depth
        tg_ = f"_n{l}"
        phase_barrier(start)
        gb = A("gb" + tg_, [128, D], F32)
        idf = A("idf" + tg_, [128, 128], F32); idb = A("idb" + tg_, [128, 128], BF16)
        xr = [A(f"xr{i}" + tg_, [128, D], F32) for i in range(2)]
        dl = [A(f"dl{i}" + tg_, [128, D], F32) for i in range(2)]
        sq = A("sq" + tg_, [128, D], BF16)
        ss = [A(f"ss{i}" + tg_, [128, 1], F32) for i in range(2)]
        rs = [A(f"rs{i}" + tg_, [128, 1], F32) for i in range(2)]
        yo = [A(f"yo{i}" + tg_, [128, D], F32 if final else BF16) for i in range(2)]
        xT = [A(f"xT{i}" + tg_, [128, KC, 128], BF16) for i in range(2)]
        tpv = [banks[i][:, 0:256].bitcast(BF16).rearrange("p (a b) -> p a b", a=4) for i in range(2)]
        s_c = Sm("ncst"); s_l = [Sm("nl0"), Sm("nl1")]; s_e = Sm("ne"); s_x = [Sm("nx0"), Sm("nx1")]; s_y = [Sm("ny0"), Sm("ny1")]
        s_tp = Sm("ntp"); s_te = Sm("nte"); s_ts = [Sm("nts0"), Sm("nts1")]
        P.op("sp", lambda e: e.dma_start(out=gb[:], in_=gN[l:l + 1, :].broadcast_to([128, D])), inc=s_c, by=16)
        v_c = P.op("sp", lambda e: e.dma_start(out=idf[:], in_=ident[:, :]), inc=s_c, by=16)
        v_id = P.op("dve", lambda e: e.tensor_copy(out=idb[:], in_=idf[:]), waits=[(s_c, v_c)], inc=s_e)
        src = x_sh if l <= 1 else xcur
        e_hist = {}
        y1_v, xs_v, ys_v, te_v, ts_v = [], [0, 0], [0, 0], [], [0, 0]
        tev_last = [0, 0]; grp = 0
        for t in range(NT):
            b = t % 2; r_ = slice(t * 128, (t + 1) * 128)
            w_reuse = [(s_e, y1_v[t - 2])] if t >= 2 else []
            if xs_v[b]:
                w_reuse.append((s_x[b], xs_v[b]))
            v_l = P.op("sp", lambda e, b=b, r_=r_: e.dma_start(out=xr[b][:], in_=src[r_, :]), waits=w_reuse, inc=s_l[b], by=16)
            if l >= 1:
                v_l = P.op("sp", lambda e, b=b, r_=r_: e.dma_start(out=dl[b][:].rearrange("p (c w) -> p c w", c=NCR),
                           in_=delta.ap().rearrange("(c t) w -> t c w", c=NCR)[r_, :, :]), inc=s_l[b], by=16)
                w_ = [(s_l[b], v_l)] + ([(s_x[b], xs_v[b])] if xs_v[b] else [])
                v = P.op("dve", lambda e, b=b: e.tensor_tensor(out=xr[b][:], in0=xr[b][:], in1=dl[b][:], op=ALU.add), waits=w_, inc=s_e)
                if not final:
                    xs_v[b] = P.op("sp", lambda e, b=b, r_=r_: e.dma_start(out=xcur[r_, :], in_=xr[b][:]), waits=[(s_e, v)], inc=s_x[b], by=16)
                w_sq = [(s_e, v)]
            else:
                w_sq = [(s_l[b], v_l)]
            if t >= 1:
                w_sq.append((s_e, e_hist[("sq", t - 1)]))
            if t >= 2:
                w_sq.append((s_e, e_hist[("r1", t - 2)]))
            v = P.op("act", lambda e, b=b: e.activation(out=sq[:], in_=xr[b][:], func=AF.Square, accum_out=ss[b][:]), waits=w_sq, inc=s_e)
            e_hist[("sq", t)] = v
            v = P.op("dve", lambda e, b=b: e.tensor_scalar(out=rs[b][:], in0=ss[b][:], scalar1=1.0 / D, scalar2=EPS, op0=ALU.mult, op1=ALU.add),
                     waits=[(s_e, v)] + ([(s_e, y1_v[t - 2])] if t >= 2 else []), inc=s_e)
            e_hist[("r1", t)] = v
            v = P.op("act", lambda e, b=b: e.activation(out=rs[b][:], in_=rs[b][:], func=AF.Sqrt), waits=[(s_e, v)], inc=s_e)
            v = P.op("dve", lambda e, b=b: e.reciprocal(out=rs[b][:], in_=rs[b][:]), waits=[(s_e, v)], inc=s_e)
            w_ = [(s_e, v), (s_c, v_c)]
            if final and ys_v[b]: w_.append((s_y[b], ys_v[b]))
            if (not final) and t >= 2: w_.append((s_tp, e_hist[("tp", t - 2)]))
            if l >= 1 and not final and xs_v[b]: w_.append((s_x[b], xs_v[b]))
            v = P.op("dve", lambda e, b=b: e.scalar_tensor_tensor(out=yo[b][:], in0=xr[b][:], scalar=rs[b][:, 0:1], in1=gb[:], op0=ALU.mult, op1=ALU.mult),
                     waits=w_, inc=s_e)
            y1_v.append(v)
            if final:
                ys_v[b] = P.op("sp", lambda e, b=b, r_=r_: e.dma_start(out=out[r_, :], in_=yo[b][:]), waits=[(s_e, v)], inc=s_y[b], by=16)
                continue
            for k0 in range(0, KC, 4):
                pb = grp % 2
                for j in range(4):
                    w_ = []
                    if j == 0:
                        w_ = [(s_e, v), (s_e, v_id)]
                        if tev_last[pb]: w_.append((s_te, tev_last[pb]))
                    v_tp = P.op("pe", lambda e, pb=pb, j=j, b=b, k=k0 + j: e.transpose(out=tpv[pb][:, j, :], in_=yo[b][:, k * 128:(k + 1) * 128], identity=idb[:]),
                                waits=w_, inc=s_tp)
                w_ = [(s_tp, v_tp)]
                if k0 == 0 and ts_v[b]: w_.append((s_ts[b], ts_v[b]))
                v_te = P.op("act", lambda e, pb=pb, k0=k0, b=b: e.activation(out=xT[b][:, k0:k0 + 4, :], in_=tpv[pb][:, :, :], func=AF.Copy), waits=w_, inc=s_te)
                tev_last[pb] = v_te; grp += 1
            e_hist[("tp", t)] = v_tp
            ts_v[b] = P.op("sp", lambda e, b=b, t=t: e.dma_start(
                out=xnT_sh.ap().rearrange("(kc p) t -> p kc t", p=128)[:, :, t * 128:(t + 1) * 128], in_=xT[b][:]),
                waits=[(s_te, v_te)], inc=s_ts[b], by=16)
        done = [(s_e, s_e.n), (s_tp, s_tp.n), (s_te, s_te.n), (s_c, s_c.n)]
        for sl in (s_l, s_x, s_y, s_ts):
            done += [(sl[0], sl[0].n), (sl[1], sl[1].n)]
        end_phase()
        return done

    def coll(kind, op, rg, src_ap, dst_ap, start):
        v = P.op("pool", lambda e: e.collective_compute(kind, op, replica_groups=rg, ins=[src_ap.opt()], outs=[dst_ap.opt()]),
                 waits=list(start), inc=s_cc, by=1)
        return [(s_cc, v)]

    def allgather_xnT(start):
        d = start
        for j in range(NCA):
            d = coll("AllGather", ALU.bypass, RG2X, xnT_sh[j * CA:(j + 1) * CA, :], ag2[j * 2 * CA:(j + 1) * 2 * CA, :], d)
            d = coll("AllGather", ALU.bypass, RG4, ag2[j * 2 * CA:(j + 1) * 2 * CA, :], xnT_g[j * 8 * CA:(j + 1) * 8 * CA, :], d)
        return d

    def reducescatter_partial(start):
        d = start
        for c in range(NCR):
            d = coll("ReduceScatter", ALU.add, RG4, partial[c * S:(c + 1) * S, :], rs4[c * (S // 4):(c + 1) * (S // 4), :], d)
            d = coll("ReduceScatter", ALU.add, RG2X, rs4[c * (S // 4):(c + 1) * (S // 4), :], delta[c * T:(c + 1) * T, :], d)
        return d

    def oproj_phase(l, mixed, start):
        tg_ = f"_o{l}"
        phase_barrier(start)
        idf = A("idf" + tg_, [128, 128], F32); idb = A("idb" + tg_, [128, 128], BF16)
        wb = A("wb" + tg_, [128, 4, D], BF16)
        mf = [A(f"mf{i}" + tg_, [128, 512], F32) for i in range(2)]
        mb = [A(f"mb{i}" + tg_, [128, 512], BF16) for i in range(2)]
        mT = [A(f"mT{i}" + tg_, [128, 4, 128], BF16) for i in range(2)]
        ot = [A(f"ot{i}" + tg_, [128, 512], F32) for i in range(2)]
        tpv = [banks[i][:, 0:256].bitcast(BF16).rearrange("p (a b) -> p a b", a=4) for i in range(2)]
        pso = [banks[2], banks[3]]
        s_c = Sm("oc"); s_l = [Sm("ol0"), Sm("ol1")]; s_e = Sm("oe"); s_tp = Sm("otp"); s_mm = Sm("omm"); s_ev = Sm("oev"); s_s = [Sm("os0"), Sm("os1")]
        v_c = P.op("sp", lambda e: e.dma_start(out=idf[:], in_=ident[:, :]), inc=s_c, by=16)
        for k in range(4):
            for h0 in range(0, D, 2048):
                P.op("pool", lambda e, k=k, h0=h0: e.dma_start(out=wb[:, k, h0:h0 + 2048], in_=wo[l, k * 128:(k + 1) * 128, h0:h0 + 2048]), inc=s_c, by=16)
        v_w = s_c.n
        v_id = P.op("dve", lambda e: e.tensor_copy(out=idb[:], in_=idf[:]), waits=[(s_c, v_w)], inc=s_e)
        cast_v, tpl_v, tev_v, ev_v, st_v = [], [], [], [], [0, 0]
        it = 0
        for t in range(S // 128):
            b = t % 2; r_ = slice(t * 128, (t + 1) * 128)
            v_l = P.op("sp", lambda e, b=b, r_=r_: e.dma_start(out=mf[b][:], in_=mixed[r_, :]), waits=[(s_e, cast_v[t - 2])] if t >= 2 else [], inc=s_l[b], by=16)
            v_cst = P.op("dve", lambda e, b=b: e.tensor_copy(out=mb[b][:], in_=mf[b][:]),
                         waits=[(s_l[b], v_l)] + ([(s_tp, tpl_v[t - 2])] if t >= 2 else []), inc=s_e)
            cast_v.append(v_cst)
            for j in range(4):
                w_ = []
                if j == 0:
                    w_ = [(s_e, v_cst), (s_e, v_id)] + ([(s_e, tev_v[t - 2])] if t >= 2 else [])
                v_tp = P.op("pe", lambda e, b=b, j=j: e.transpose(out=tpv[b][:, j, :], in_=mb[b][:, j * 128:(j + 1) * 128], identity=idb[:]), waits=w_, inc=s_tp)
            tpl_v.append(v_tp)
            v_te = P.op("act", lambda e, b=b: e.activation(out=mT[b][:], in_=tpv[b][:, :, :], func=AF.Copy),
                        waits=[(s_tp, v_tp)] + ([(s_mm, mm_last[t - 2])] if t >= 2 else []), inc=s_e)
            tev_v.append(v_te)
            if t == 0:
                mm_last = {}
            for nb in range(D // 512):
                pb = it % 2
                for k in range(4):
                    w_ = []
                    if k == 0:
                        w_ = [(s_e, v_te), (s_c, v_w)] + ([(s_ev, ev_v[it - 2])] if it >= 2 else [])
                    v_mm = P.op("pe", lambda e, pb=pb, k=k, b=b, nb=nb: e.matmul(pso[pb][:], lhsT=mT[b][:, k, :], rhs=wb[:, k, nb * 512:(nb + 1) * 512],
                                                                           start=(k == 0), stop=(k == 3)), waits=w_, inc=s_mm)
                w_ = [(s_mm, v_mm)] + ([(s_s[pb], st_v[pb])] if st_v[pb] else [])
                eng = "act" if it % 2 == 0 else "dve"
                if eng == "act":
                    v_ev = P.op("act", lambda e, pb=pb: e.activation(out=ot[pb][:], in_=pso[pb][:], func=AF.Copy), waits=w_, inc=s_ev)
                else:
                    v_ev = P.op("dve", lambda e, pb=pb: e.tensor_copy(out=ot[pb][:], in_=pso[pb][:]), waits=w_, inc=s_ev)
                ev_v.append(v_ev)
                for i4 in range(512 // CW):
                    c_ = nb * (512 // CW) + i4
                    st_v[pb] = P.op("sp", lambda e, pb=pb, t=t, c_=c_, i4=i4: e.dma_start(
                        out=partial[c_ * S + t * 128: c_ * S + (t + 1) * 128, :], in_=ot[pb][:, i4 * CW:(i4 + 1) * CW]),
                        waits=[(s_ev, v_ev)] if i4 == 0 else [], inc=s_s[pb], by=16)
                it += 1
            mm_last[t] = v_mm
        done = [(s_e, s_e.n), (s_tp, s_tp.n), (s_mm, s_mm.n), (s_ev, s_ev.n), (s_c, s_c.n),
                (s_l[0], s_l[0].n), (s_l[1], s_l[1].n), (s_s[0], s_s[0].n), (s_s[1], s_s[1].n)]
        end_phase()
        return done

    done = []
    for l in range(depth):
        done = norm_phase(l, done)
        done = allgather_xnT(done)

        def xnT_tile(k, tt, _T=T):
            q, t0 = (tt * 512) // _T, (tt * 512) % _T
            j, r0 = k // 2, (k % 2) * 128
            base = (j * 8 + q) * CA + r0
            return xnT_g[base:base + 128, t0:t0 + 512]
        io = {"xnT": xnT_tile, "wc": wc[l], "biasT": biasT[l], "bp": bp[l], "wup": wup[l], "aup": aup[l], "CS": CS, "SS": SS, "FS": FS, "M2": M2,
              "tg": tg, "identA": ident, "identB": ident, "BD": BD, "start": done}
        build_B(S, 5, ctx=ctx, io=io, tag=f"_L{l}")
        done = oproj_phase(l, io["mixed"], io["done"])
        done = reducescatter_partial(done)
    done = norm_phase(depth, done)
    P.op("sp", lambda e: e.nop(), waits=done)
    P.emit()
    return nc


SEQ, DEPTH, NCORES = 8192, 4, 8
_CH, _BACK, _CLIP = 64, 8, 128


def _bias_tiles(rb):
    q = np.arange(128)[None, :]
    out = np.empty((5, 128, 128), np.float32)
    for j in range(5):
        k = (j - 4) * 128 + np.arange(128)[:, None]
        qc_, kc_ = q // _CH, np.floor_divide(k, _CH)
        valid = (kc_ <= qc_) & (kc_ >= qc_ - _BACK)
        out[j] = np.where(valid, rb[np.clip(q - k, -(_CH - 1), _CLIP) + (_CH - 1)], -30000.0)
    return out


_PROG = {}


def kernel(x, norm_g, w_in, w_out, b_mu, b_w0, b_w_up, b_a0, b_a_up, b_k_k, b_k_a, b_r_k, b_ln_g, b_ln_b, c_rel_bias, final_g):
    f = np.float32
    A_ = lambda a: np.asarray(a, f)
    x = A_(x); S = x.shape[1]; T = S // NCORES; L = DEPTH
    norm_g, w_in, w_out, b_mu, b_w0, b_w_up, b_a0, b_a_up = map(A_, (norm_g, w_in, w_out, b_mu, b_w0, b_w_up, b_a0, b_a_up))
    b_k_k, b_k_a, b_r_k, b_ln_g, b_ln_b, c_rel_bias, final_g = map(A_, (b_k_k, b_k_a, b_r_k, b_ln_g, b_ln_b, c_rel_bias, final_g))
    inv = (1.0 / (10000.0 ** (np.arange(0, 128, 2, dtype=f) / 128))).astype(f)
    ang = np.arange(S, dtype=f)[None, :] * inv[:, None]
    cc_, ss_ = np.cos(ang).astype(f), np.sin(ang).astype(f)
    CS, SS = np.concatenate([cc_, cc_], 0), np.concatenate([-ss_, ss_], 0)
    ident = np.eye(128, dtype=f); BD = np.kron(np.eye(2, dtype=f), np.ones((64, 64), f))
    gN = np.ascontiguousarray(np.concatenate([norm_g[:L], final_g[None]], 0))
    maps = []
    for c in range(NCORES):
        sh = shard_of_core(c); ha = c // 2
        lg = np.log(np.float64(1.0 - 2.0 ** (-5.0 - ha)))
        mm_ = np.arange(128)[:, None]; nn_ = np.arange(128)[None, :]; same = (mm_ // 64) == (nn_ // 64)
        ci = np.concatenate(list(core_cols(c).values())); mc = mixed_cols(c)
        ch = c * 128 + np.arange(128); hd = ch // 64; jj = ch % 64; r128 = np.arange(128)
        maps.append({
            "x": np.ascontiguousarray(x[0, sh * T:(sh + 1) * T]), "gN": gN,
            "wc": np.ascontiguousarray(w_in[:L][:, :, ci]), "wo": np.ascontiguousarray(w_out[:L][:, mc, :]),
            "biasT": np.ascontiguousarray(np.stack([np.stack([_bias_tiles(c_rel_bias[l][2 * c + hh]) for hh in range(2)], 0).transpose(2, 0, 1, 3)
                                                    for l in range(L)], 0)),
            "bp": np.stack([np.stack([b_mu[l][ch], b_mu[l][1024 + ch], b_mu[l][2048 + ch], b_mu[l][3072 + r128], b_mu[l][3200 + r128],
                                      b_w0[l][ch], b_a0[l][ch], b_k_k[l][ch], b_k_a[l][ch], b_r_k[l][hd, jj], b_ln_g[l][ch], b_ln_b[l][ch]], 1)
                            for l in range(L)], 0).astype(f),
            "wup": np.ascontiguousarray(b_w_up[:L][:, :, ch]), "aup": np.ascontiguousarray(b_a_up[:L][:, :, ch]),
            "CS": CS, "SS": SS,
            "FS": np.broadcast_to(np.exp(lg * (np.arange(128) + 1.0)).astype(f)[None, :], (128, 128)).copy(),
            "M2": np.where(same, np.exp(lg * np.abs(nn_ - mm_)), np.where(mm_ < nn_, np.exp(lg * (nn_ - mm_)), 0.0)).astype(f),
            "tg": np.stack([np.exp(lg * (127.0 - np.arange(128))), np.full(128, np.exp(lg * 128))], 1).astype(f),
            "ident": ident, "BD": BD})
    if (S, L) not in _PROG:
        _PROG[(S, L)] = build_F(S, L)
    r = run_bass_kernel_spmd(_PROG[(S, L)], maps, core_ids=list(range(NCORES)))
    out = np.empty((1, S, x.shape[2]), f)
    for c in range(NCORES):
        sh = shard_of_core(c)
        out[0, sh * T:(sh + 1) * T] = r.results[c]["out"]
    return out
```
